# Optimizing a Trainium2 kernel written in Bass

```python
import math
import jax, jax.numpy as jnp
from jax import lax
import numpy as np

D_MODEL = 4096
BATCH = 4
SEQ = 2048
DEPTH = 1
DEC_BATCH = 32
DEC_SEQ = 1
PAST_LEN = 8192
PAGE_SIZE = 128

HEAD_DIM = 128
N_GROUPS_A = 3
HEADS_PER_GROUP = 8
N_HEADS_A = N_GROUPS_A * HEADS_PER_GROUP
WINDOWS = (128, 512, 2048)
DILATIONS = (1, 4, 16)
QBLOCK = 128
A_OUT = HEADS_PER_GROUP * HEAD_DIM
QKV_WIDTH = 3 * N_HEADS_A * HEAD_DIM
POOL_WINDOWS = (2, 4, 8, 16)
N_POOL_GROUPS = len(POOL_WINDOWS)
POOL_WIDTH = D_MODEL // 2
POOL_GROUP_WIDTH = POOL_WIDTH // N_POOL_GROUPS
POOL_OUT_GROUP = D_MODEL // N_POOL_GROUPS
POOL_HIST = max(POOL_WINDOWS) - 1
IN_WIDTH = QKV_WIDTH + POOL_WIDTH + 2 * D_MODEL
D_FF = -((-8 * D_MODEL) // (3 * 256)) * 256
RMS_EPS = 1e-6
ALIBI_MAX_EXP = 8.0

kernel_name = 'dilated_pool_gated_hybrid_step'


def rms_norm(x, g):
    xf = x.astype(jnp.float32)
    y = xf * lax.rsqrt(jnp.mean(xf * xf, axis=-1, keepdims=True) + RMS_EPS)
    return (y * g.astype(jnp.float32)).astype(x.dtype)


def alibi_slopes():
    s = np.array([2.0 ** (-ALIBI_MAX_EXP * (i + 1) / N_HEADS_A) for i in range(N_HEADS_A)], dtype=np.float32)
    return jnp.asarray(s.reshape(N_GROUPS_A, HEADS_PER_GROUP))


def project_inputs(h, w_in):
    b, t, _ = h.shape
    z = jnp.einsum('btd,de->bte', h, w_in)
    qkv = z[..., :QKV_WIDTH].reshape(b, t, 3, N_GROUPS_A, HEADS_PER_GROUP, HEAD_DIM)
    u = z[..., QKV_WIDTH:QKV_WIDTH + POOL_WIDTH]
    gates = z[..., QKV_WIDTH + POOL_WIDTH:].reshape(b, t, 2, D_MODEL)
    return qkv[:, :, 0], qkv[:, :, 1], qkv[:, :, 2], u, gates


def dilated_band_attention(q, k, v, dil, band, slopes):
    b, s_len, h, dh = q.shape
    n_sub = s_len // dil
    n_blk = -(-n_sub // QBLOCK)
    l_pad = n_blk * QBLOCK

    def to_sub(a):
        a = a.reshape(b, n_sub, dil, h, dh).transpose(0, 2, 1, 3, 4)
        return jnp.pad(a, ((0, 0), (0, 0), (0, l_pad - n_sub), (0, 0), (0, 0)))

    def windows(a):
        a = jnp.pad(a, ((0, 0), (0, 0), (QBLOCK, 0), (0, 0), (0, 0)))
        a = a.reshape(b, dil, n_blk + 1, QBLOCK, h, dh)
        return jnp.concatenate([a[:, :, :-1], a[:, :, 1:]], axis=3)

    qb = to_sub(q).reshape(b, dil, n_blk, QBLOCK, h, dh)
    kw = windows(to_sub(k))
    vw = windows(to_sub(v))
    scores = jnp.einsum('brnqhd,brnkhd->brnhqk', qb, kw,
                        preferred_element_type=jnp.float32) * (HEAD_DIM ** -0.5)
    q_idx = jnp.arange(QBLOCK)
    k_idx = jnp.arange(2 * QBLOCK) - QBLOCK
    delta = q_idx[:, None] - k_idx[None, :]
    k_sub = (jnp.arange(n_blk) * QBLOCK)[:, None, None] + k_idx[None, None, :]
    valid = (delta >= 0) & (delta <= band) & (k_sub >= 0)
    bias = -slopes[:, None, None] * (dil * delta).astype(jnp.float32)[None]
    scores = jnp.where(valid[None, None, :, None], scores + bias[None, None, None], -jnp.inf)
    m = jnp.max(scores, axis=-1)
    p = jnp.exp(scores - m[..., None])
    den = jnp.sum(p, axis=-1)
    num = jnp.einsum('brnhqk,brnkhd->brnqhd', p, vw.astype(jnp.float32))

    def from_sub(a):
        a = a.reshape((b, dil, l_pad) + a.shape[4:])[:, :, :n_sub]
        a = jnp.moveaxis(a, 1, 2)
        return a.reshape((b, s_len) + a.shape[3:])

    return from_sub(num), from_sub(jnp.swapaxes(m, 3, 4)), from_sub(jnp.swapaxes(den, 3, 4))


def dilated_cached_attention(q, k_new, v_new, k_buf, v_buf, dil, band, slopes):
    bd, t_new, h, dh = q.shape
    l_buf = k_buf.shape[1]
    kf = jnp.concatenate([k_buf, k_new.astype(k_buf.dtype)], axis=1)
    vf = jnp.concatenate([v_buf, v_new.astype(v_buf.dtype)], axis=1)
    steps = jnp.arange(band + 1)
    idx = l_buf + jnp.arange(t_new)[:, None] - dil * steps[None, :]
    valid = idx >= 0
    flat = jnp.clip(idx, 0).reshape(-1)
    kg = jnp.take(kf, flat, axis=1).reshape(bd, t_new, band + 1, h, dh)
    vg = jnp.take(vf, flat, axis=1).reshape(bd, t_new, band + 1, h, dh)
    scores = jnp.einsum('bthd,btkhd->bthk', q, kg,
                        preferred_element_type=jnp.float32) * (HEAD_DIM ** -0.5)
    scores = scores - (slopes[:, None] * (dil * steps).astype(jnp.float32)[None, :])[None, None]
    scores = jnp.where(valid[None, :, None, :], scores, -jnp.inf)
    m = jnp.max(scores, axis=-1)
    p = jnp.exp(scores - m[..., None])
    den = jnp.sum(p, axis=-1)
    num = jnp.einsum('bthk,btkhd->bthd', p, vg.astype(jnp.float32))
    return num, m, den, kf[:, t_new:], vf[:, t_new:]


def multi_scale_pool(u_hist, u_new, start_pos):
    b, t_new, _ = u_new.shape
    u = jnp.concatenate([u_hist.astype(jnp.float32), u_new.astype(jnp.float32)], axis=1)
    cs = jnp.concatenate([jnp.zeros((b, 1, POOL_WIDTH), jnp.float32), jnp.cumsum(u, axis=1)], axis=1)
    pos = start_pos + jnp.arange(t_new)
    un = u_new.astype(jnp.float32)
    outs = []
    for g, w in enumerate(POOL_WINDOWS):
        sl = slice(g * POOL_GROUP_WIDTH, (g + 1) * POOL_GROUP_WIDTH)
        tot = cs[:, POOL_HIST + 1:, sl] - cs[:, POOL_HIST + 1 - w:POOL_HIST + 1 - w + t_new, sl]
        cnt = jnp.minimum(w, pos + 1).astype(jnp.float32)
        outs.append(tot / cnt[None, :, None] - un[..., sl])
    return jnp.stack(outs, axis=2)


def finish_layer(x, attn_parts, pooled, gates, w_up_attn, w_pool_map, pool_scale, w_out,
                 norm_ffn, w_ffn_gate, w_ffn_up, w_ffn_down):
    b, t, _ = x.shape
    num = jnp.stack([p[0] for p in attn_parts])
    m = jnp.stack([p[1] for p in attn_parts])
    den = jnp.stack([p[2] for p in attn_parts])
    e = jnp.exp(m - jnp.max(m, axis=0, keepdims=True))
    attn = jnp.sum(e[..., None] * num, axis=0) / jnp.sum(e * den, axis=0)[..., None]
    a = jnp.einsum('bte,ed->btd', attn.reshape(b, t, A_OUT).astype(x.dtype), w_up_attn)
    pb = jnp.einsum('btgc,gce->btge', pooled.astype(x.dtype), w_pool_map).reshape(b, t, D_MODEL) * pool_scale
    mix = jax.nn.sigmoid(gates[:, :, 0]) * a + jax.nn.sigmoid(gates[:, :, 1]) * pb
    x = x + jnp.einsum('btd,de->bte', mix, w_out)
    hf = rms_norm(x, norm_ffn)
    ff = jax.nn.silu(jnp.einsum('btd,df->btf', hf, w_ffn_gate)) * jnp.einsum('btd,df->btf', hf, w_ffn_up)
    return x + jnp.einsum('btf,fd->btd', ff, w_ffn_down)


def setup_inputs(seed: int = 0) -> dict:
    key = jax.random.key(seed)
    ks = jax.random.split(key, 20)
    f32 = jnp.float32

    def nrm(k, shape, scale=1.0):
        return jax.random.normal(k, shape, f32) * scale

    def cache_shape(w):
        return (DEPTH, DEC_BATCH, min(w, PAST_LEN), HEADS_PER_GROUP, HEAD_DIM)

    return {
        'x_prompt': nrm(ks[0], (BATCH, SEQ, D_MODEL)),
        'x_sample': nrm(ks[1], (DEC_BATCH, DEC_SEQ, D_MODEL)),
        'cache_k_w128': nrm(ks[2], cache_shape(WINDOWS[0])),
        'cache_v_w128': nrm(ks[3], cache_shape(WINDOWS[0])),
        'cache_k_w512': nrm(ks[4], cache_shape(WINDOWS[1])),
        'cache_v_w512': nrm(ks[5], cache_shape(WINDOWS[1])),
        'cache_k_w2048': nrm(ks[6], cache_shape(WINDOWS[2])),
        'cache_v_w2048': nrm(ks[7], cache_shape(WINDOWS[2])),
        'state_pool': nrm(ks[8], (DEPTH, DEC_BATCH, POOL_HIST, POOL_WIDTH)),
        'norm_mix': 1.0 + nrm(ks[9], (DEPTH, D_MODEL), 0.02),
        'w_in': nrm(ks[10], (DEPTH, D_MODEL, IN_WIDTH), D_MODEL ** -0.5),
        'w_up_attn': nrm(ks[11], (DEPTH, A_OUT, D_MODEL), A_OUT ** -0.5),
        'w_pool_map': nrm(ks[12], (DEPTH, N_POOL_GROUPS, POOL_GROUP_WIDTH, POOL_OUT_GROUP), POOL_GROUP_WIDTH ** -0.5),
        'pool_scale': 1.0 + nrm(ks[13], (DEPTH, D_MODEL), 0.02),
        'w_out': nrm(ks[14], (DEPTH, D_MODEL, D_MODEL), D_MODEL ** -0.5),
        'norm_ffn': 1.0 + nrm(ks[15], (DEPTH, D_MODEL), 0.02),
        'w_ffn_gate': nrm(ks[16], (DEPTH, D_MODEL, D_FF), D_MODEL ** -0.5),
        'w_ffn_up': nrm(ks[17], (DEPTH, D_MODEL, D_FF), D_MODEL ** -0.5),
        'w_ffn_down': nrm(ks[18], (DEPTH, D_FF, D_MODEL), D_FF ** -0.5),
        'norm_final': 1.0 + nrm(ks[19], (D_MODEL,), 0.02),
    }


def reference(x_prompt, x_sample, cache_k_w128, cache_v_w128, cache_k_w512, cache_v_w512,
              cache_k_w2048, cache_v_w2048, state_pool, norm_mix, w_in, w_up_attn, w_pool_map,
              pool_scale, w_out, norm_ffn, w_ffn_gate, w_ffn_up, w_ffn_down, norm_final):
    slopes = alibi_slopes()
    k_caches = (cache_k_w128, cache_k_w512, cache_k_w2048)
    v_caches = (cache_v_w128, cache_v_w512, cache_v_w2048)
    xp, xs = x_prompt, x_sample
    n_prompt, seq_len = x_prompt.shape[0], x_prompt.shape[1]
    kp_out = [[] for _ in WINDOWS]
    vp_out = [[] for _ in WINDOWS]
    ksm_out = [[] for _ in WINDOWS]
    vsm_out = [[] for _ in WINDOWS]
    pool_p_out, pool_s_out = [], []
    for l in range(DEPTH):
        hp = rms_norm(xp, norm_mix[l])
        hs = rms_norm(xs, norm_mix[l])
        qp, kp, vp, up, gp = project_inputs(hp, w_in[l])
        qs, ksm, vsm, us, gs = project_inputs(hs, w_in[l])
        parts_p, parts_s = [], []
        for g in range(N_GROUPS_A):
            dil = DILATIONS[g]
            band = WINDOWS[g] // dil
            parts_p.append(dilated_band_attention(qp[:, :, g], kp[:, :, g], vp[:, :, g], dil, band, slopes[g]))
            num, m, den, kb, vb = dilated_cached_attention(
                qs[:, :, g], ksm[:, :, g], vsm[:, :, g], k_caches[g][l], v_caches[g][l], dil, band, slopes[g])
            parts_s.append((num, m, den))
            rows = min(WINDOWS[g], seq_len)
            kp_out[g].append(kp[:, seq_len - rows:, g])
            vp_out[g].append(vp[:, seq_len - rows:, g])
            ksm_out[g].append(kb)
            vsm_out[g].append(vb)
        pooled_p = multi_scale_pool(jnp.zeros((n_prompt, POOL_HIST, POOL_WIDTH), up.dtype), up, 0)
        pooled_s = multi_scale_pool(state_pool[l], us, PAST_LEN)
        pool_p_out.append(up[:, seq_len - POOL_HIST:])
        pool_s_out.append(jnp.concatenate([state_pool[l], us.astype(state_pool.dtype)], axis=1)[:, -POOL_HIST:])
        xp = finish_layer(xp, parts_p, pooled_p, gp, w_up_attn[l], w_pool_map[l], pool_scale[l], w_out[l],
                          norm_ffn[l], w_ffn_gate[l], w_ffn_up[l], w_ffn_down[l])
        xs = finish_layer(xs, parts_s, pooled_s, gs, w_up_attn[l], w_pool_map[l], pool_scale[l], w_out[l],
                          norm_ffn[l], w_ffn_gate[l], w_ffn_up[l], w_ffn_down[l])
    y_prompt = rms_norm(xp, norm_final)
    y_sample = rms_norm(xs, norm_final)
    return (y_prompt, y_sample,
            jnp.stack(kp_out[0]), jnp.stack(vp_out[0]), jnp.stack(kp_out[1]), jnp.stack(vp_out[1]),
            jnp.stack(kp_out[2]), jnp.stack(vp_out[2]), jnp.stack(pool_p_out),
            jnp.stack(ksm_out[0]), jnp.stack(vsm_out[0]), jnp.stack(ksm_out[1]), jnp.stack(vsm_out[1]),
            jnp.stack(ksm_out[2]), jnp.stack(vsm_out[2]), jnp.stack(pool_s_out))
```

```python
import numpy as np
import concourse.bass as bass
import concourse.mybir as mybir
from concourse.bass_utils import run_bass_kernel_spmd

F32 = mybir.dt.float32
BF16 = mybir.dt.bfloat16
AF = mybir.ActivationFunctionType
ALU = mybir.AluOpType
AX = mybir.AxisListType

D = 4096
KC = 32
TP = 1024
NS = 4
T = TP + NS
TH = 1024
DFF = 11008
FC = DFF // 128
HALO = (128, 512, 1024)
DIL = (1, 4, 16)
WIN = (128, 512, 2048)
SCALE = 128.0 ** -0.5
BIG = 1.0e6
SLOPES = [[2.0 ** (-8.0 * (g * 8 + h + 1) / 24.0) for h in range(8)] for g in range(3)]
OWN_PIECES = ((0, 343), (343, 343), (686, 342))
TBLK = [(i * 128, 128) for i in range(8)] + [(1024, 4)]
THIRDS = ((0, 29), (29, 29), (58, 28))


class Tok:
    __slots__ = ("w", "r", "dsem", "dcnt", "multi", "excl")

    def __init__(self, multi=False, excl=False):
        self.excl = excl
        self.w = {}
        self.r = {}
        self.dsem = None
        self.dcnt = 0
        self.multi = multi


class Eng:
    def __init__(self, nc, eng, name, same_wait):
        self.eng = eng
        self.sem = nc.alloc_semaphore("sem_" + name)
        self.cnt = 0
        self.seen = {}
        self.same_wait = same_wait

    def pending(self, dep, raw=True):
        sem, c = dep
        if sem is self.sem and not (self.same_wait and raw):
            return False
        return self.seen.get(id(sem), 0) < c

    def need(self, dep):
        if self.pending(dep):
            sem, c = dep
            self.eng.wait_ge(sem, c)
            self.seen[id(sem)] = c

    def wait_all(self, deps_raw, deps_other, ins_fn):
        todo = {}
        for d in deps_raw:
            if self.pending(d, True):
                k = id(d[0])
                if k not in todo or todo[k][1] < d[1]:
                    todo[k] = d
        for d in deps_other:
            if self.pending(d, True):
                k = id(d[0])
                if k not in todo or todo[k][1] < d[1]:
                    todo[k] = d
        lst = list(todo.values())
        for (sem, c) in lst[:-1]:
            self.eng.wait_ge(sem, c)
            self.seen[id(sem)] = c
        ins = ins_fn()
        if lst:
            sem, c = lst[-1]
            ins._wait_ge(sem, c)
            self.seen[id(sem)] = c
        return ins


class FW:
    def __init__(self, nc):
        self.nc = nc
        self.PE = Eng(nc, nc.tensor, "pe", False)
        self.ACT = Eng(nc, nc.scalar, "act", True)
        self.DVE = Eng(nc, nc.vector, "dve", True)
        self.POOL = Eng(nc, nc.gpsimd, "pool", True)
        self.SP = Eng(nc, nc.sync, "sp", True)
        self.nsem = 0
        self.dtoks = []

    @staticmethod
    def _deps(rt, wt):
        raw, other = [], []
        for t in rt:
            raw.extend(t.w.values())
            if t.excl:
                other.extend(t.r.values())
        for t in wt:
            other.extend(t.w.values())
            other.extend(t.r.values())
        return raw, other

    def op(self, e, fn, rt=(), wt=(), signal=True):
        raw, other = self._deps(rt, wt)
        ins = e.wait_all(raw, other, fn)
        if signal:
            e.cnt += 1
            ins.then_inc(e.sem, 1)
            stamp = (e.sem, e.cnt)
        else:
            stamp = (e.sem, e.cnt + 1)
        for t in wt:
            t.w = {id(e.sem): stamp}
            t.r = {}
        for t in rt:
            t.r[id(e.sem)] = stamp
        return ins

    def dma(self, q, out, in_, rt=(), wt=(), st=None):
        if st is None:
            st = wt[0] if wt else rt[0]
        if st.dsem is None:
            st.dsem = self.nc.alloc_semaphore("dsem%d" % self.nsem)
            self.nsem += 1
            self.dtoks.append(st)
        raw, other = self._deps(rt, wt)
        st.dcnt += 16
        ins = q.wait_all(raw + other, [], lambda: q.eng.dma_start(out=out, in_=in_))
        ins.then_inc(st.dsem, 16)
        stamp = (st.dsem, st.dcnt)
        for t in wt:
            if t.multi:
                t.w[id(st.dsem)] = stamp
            else:
                t.w = {id(st.dsem): stamp}
                t.r = {}
        for t in rt:
            t.r[id(st.dsem)] = stamp


CARRY = {}


def finalize(fw):
    for t in fw.dtoks:
        fw.SP.need((t.dsem, t.dcnt))
    for e in (fw.PE, fw.ACT, fw.DVE, fw.POOL):
        if e.cnt:
            fw.SP.need((e.sem, e.cnt))


class Stack:
    def __init__(self, nc, side):
        self.nc = nc
        self.side = side
        self.items = []
        self.n = 0

    def push(self, shape, dt):
        self.n += 1
        g = self.nc.sbuf_tensor("%s%d" % (self.side[0], self.n), list(shape), dt, side=self.side)
        t = g.__enter__()
        tk = Tok()
        tk.r = dict(CARRY)
        self.items.append((g, tk))
        return t, tk

    def mark(self):
        return len(self.items)

    def pop_to(self, mark):
        while len(self.items) > mark:
            g, tk = self.items.pop()
            for dd in (tk.w, tk.r):
                for k, (sem, c) in dd.items():
                    if k not in CARRY or CARRY[k][1] < c:
                        CARRY[k] = (sem, c)
            g.__exit__(None, None, None)


def build(debug=False, stop=99):
    CARRY.clear()
    nc = bass.Bass("TRN2", target_bir_lowering=False)
    fw = FW(nc)
    PE, ACT, DVE, POOL, SP = fw.PE, fw.ACT, fw.DVE, fw.POOL, fw.SP
    op, dma = fw.op, fw.dma
    L = Stack(nc, "left")
    R = Stack(nc, "right")

    def din(name, shape):
        return nc.dram_tensor(name, list(shape), F32, kind="ExternalInput").ap()

    def dout(name, shape):
        return nc.dram_tensor(name, list(shape), F32, kind="ExternalOutput").ap()

    def dscr(name, shape, dt):
        kind = "ExternalOutput" if debug else "Internal"
        return nc.dram_tensor(name, list(shape), dt, kind=kind).ap()

    x_own = din("x_own", [TP, D]); x_smp = din("x_smp", [NS, D]); x_halo = din("x_halo", [TH, D])
    w_in = din("w_in", [D, 19456]); w_up = din("w_up", [1024, D]); w_pool = din("w_pool", [4, 512, 1024])
    if stop >= 5:
        w_out = din("w_out", [D, D]); w_gate = din("w_gate", [D, DFF]); w_upf = din("w_upf", [D, DFF])
        w_down = din("w_down", [DFF, D])
    g_mix = din("g_mix", [128, KC]); g_ffn = din("g_ffn", [128, KC]); pscale_d = din("pscale", [128, KC])
    g_fin = din("g_fin", [1, D])
    ck = [din("ck%d" % g, [NS, WIN[g] * 1024]) for g in range(3)]
    cv = [din("cv%d" % g, [NS, WIN[g] * 1024]) for g in range(3)]
    spool = din("spool", [NS * 15, 2048])
    ident_d = din("ident", [128, 128])
    Da_d = din("Da", [128, 256]); Db_d = din("Db", [128, 256]); Dc_d = din("Dc", [64, 128]); Ds_d = din("Ds", [1, 129])
    invc_d = din("invc", [128, 4, 16])

    y_own = dout("y_own", [TP, D]); y_smp = dout("y_smp", [NS, D])
    NBLK = (1, 4, 8)
    kco = [dout("kc%d" % g, [NBLK[g] * 128, 1024]) for g in range(3)]
    vco = [dout("vc%d" % g, [NBLK[g] * 128, 1024]) for g in range(3)]
    pool_p = dout("pool_p", [15, 2048])
    kso = [dout("ks%d" % g, [NS, WIN[g] * 1024]) for g in range(3)]
    vso = [dout("vs%d" % g, [NS, WIN[g] * 1024]) for g in range(3)]
    pool_s = dout("pool_s", [NS, 15 * 2048])

    kvT_d = dscr("kvT_d", [48, 128, 2052], BF16); t_kvd = [Tok() for _ in range(48)]
    mixT_d = dscr("mixT_d", [32, 128, T], BF16); t_mixd = Tok(True)
    xa_d = dscr("xa_d", [T, D], F32); t_xa = Tok(True)
    xb_d = dscr("xb_d", [T, D], F32); t_xb = Tok(True)
    t_out = Tok(True)
    t_cc = Tok(True)

    def ps(name, shape, dt):
        return nc.alloc_psum_tensor(name, list(shape), dt), Tok(excl=True)

    main = [ps("mb%d" % i, [128, 512], F32) for i in range(4)]
    _X = [nc.alloc_psum_tensor("attX%d" % i, [128, 512], F32) for i in range(2)]
    _Y = [nc.alloc_psum_tensor("attY%d" % i, [128, 512], F32) for i in range(2)]
    tX = [Tok(excl=True), Tok(excl=True)]
    tY = [Tok(excl=True), Tok(excl=True)]
    ps_s = [(_X[u][:, 0:256], tX[u]) for u in range(2)]
    ps_t = [(_X[u][:, 256:384].bitcast(BF16), tX[u]) for u in range(2)]
    ps_v = [(_X[u][:, 384:512].bitcast(BF16), tX[u]) for u in range(2)]
    ps_o = [(_Y[u][:, 0:128], tY[u]) for u in range(2)]
    ps_l = [(_Y[u][:, 128:256], tY[u]) for u in range(2)]
    ps_r = [(_Y[u][:, 256:384], tY[u]) for u in range(2)]
    ps_p = [(_Y[u][:, 384:512], tY[u]) for u in range(2)]
    ps_x = [(_X[0][:, 0:256].bitcast(BF16).rearrange("p (a b) -> p a b", b=128), tX[0])]
    mstate = [0]

    def next_bank():
        import os
        if os.environ.get('NOREUSE') and mstate[0] >= 4:
            return (_X[1], tX[1])
        b = main[mstate[0] % 4]
        mstate[0] += 1
        return b

    ident_f, t_idf = L.push([128, 128], F32)
    ident_b, t_idb = L.push([128, 128], BF16)
    ones_b, t_1b = L.push([128, 128], BF16)
    ones_f, t_1f = L.push([128, 128], F32)
    eps_t, t_eps = L.push([128, 1], F32)
    Da, t_Da = L.push([128, 256], F32); Db, t_Db = L.push([128, 256], F32)
    Dc, t_Dc = L.push([64, 128], F32); Ds, t_Ds = L.push([1, 129], F32)
    gmix, t_gm = L.push([128, KC], F32); gffn, t_gf = L.push([128, KC], F32); pscale, t_psc = L.push([128, KC], F32)
    invc, t_invc = L.push([128, 4, 16], F32)
    uhalo, t_uh = L.push([128, 16, 16], F32)
    knew = [[L.push([128, 8, NS], F32) for kv in range(2)] for g in range(3)]
    ss1, t_ss1 = L.push([128, 9, 8], F32)
    ss3, t_ss3 = L.push([128, 9, 8], F32)
    dma(SP, ident_f[:], ident_d, wt=[t_idf])
    dma(POOL, ident_b[:], ident_d, wt=[t_idb])
    op(DVE, lambda: nc.vector.memset(ones_b[:], 1.0), wt=[t_1b])
    op(DVE, lambda: nc.vector.memset(ones_f[:], 1.0), wt=[t_1f])
    op(DVE, lambda: nc.vector.memset(eps_t[:], 1e-6), wt=[t_eps])
    for (sb_, tk_, dr_) in ((Da, t_Da, Da_d), (Db, t_Db, Db_d), (Dc, t_Dc, Dc_d), (Ds, t_Ds, Ds_d),
                            (gmix, t_gm, g_mix), (gffn, t_gf, g_ffn), (pscale, t_psc, pscale_d), (invc, t_invc, invc_d)):
        dma(SP, sb_[:], dr_, wt=[tk_])

    for g in range(3):
        n_el = (WIN[g] - 1) * 1024
        for (src, dst) in ((ck[g], kso[g]), (cv[g], vso[g])):
            for b in range(NS):
                dma(ACT, dst[b, 0:n_el].rearrange("(a f) -> a f", a=16),
                    src[b, 1024:1024 + n_el].rearrange("(a f) -> a f", a=16), wt=[t_cc])
    for b in range(NS):
        dma(ACT, pool_s[b, 0:14 * 2048].rearrange("(a f) -> a f", a=16),
            spool[b * 15 + 1:b * 15 + 15, :].rearrange("r c -> (r c)").rearrange("(a f) -> a f", a=16), wt=[t_cc])

    class Slots:
        def __init__(self, n, shape):
            self.tiles = [R.push(shape, BF16) for _ in range(n)]
            self.i = 0
            self.pending = [False] * n

        def load(self, src_ap):
            k = self.i % len(self.tiles)
            self.i += 1
            assert not self.pending[k], "slot reloaded before use"
            self.pending[k] = True
            t, tk = self.tiles[k]
            dma(POOL, t[:], src_ap, wt=[tk])
            return k

        def use(self, k):
            self.pending[k] = False
            return self.tiles[k]

    def wcols(w, c0, n, r0=0, nkc=KC):
        return w[r0:r0 + nkc * 128, c0:c0 + n].rearrange("(kc p) n -> p kc n", p=128)

    class Job:
        def __init__(self, load, compute):
            self.load = load
            self.compute = compute

    def run_jobs(jobs, pf=3):
        n = len(jobs)
        loaded = 0
        for i in range(n):
            while loaded < min(n, i + pf + 1):
                jobs[loaded].slot = jobs[loaded].load() if jobs[loaded].load else None
                loaded += 1
            jobs[i].compute(jobs[i].slot)

    def gemm(wt_tile, wt_tok, nkc, movs, ep, act_toks):
        for pi, (rhs_fn, n) in enumerate(movs):
            bank, t_bank = next_bank()
            for kc in range(nkc):
                op(PE, lambda kc=kc: nc.tensor.matmul(bank[:, 0:n], lhsT=wt_tile[:, kc, :], rhs=rhs_fn(kc),
                                                      start=(kc == 0), stop=(kc == nkc - 1)),
                   rt=[wt_tok] + act_toks, wt=[t_bank], signal=(kc == nkc - 1))
            ep(pi, bank, t_bank, n)

    def norm_T(blocks, gtab, t_g, dstT, t_dst, xin, t_xin, xs, t_xs, ssq, t_ssq, rstd, t_rstd, pre_ss=None):
        for (src, n, col0, src_tok) in blocks:
            dma(SP, xin[0:n, :], src, rt=[src_tok] if src_tok else [], wt=[t_xin])
            if pre_ss is None:
                op(ACT, lambda: nc.scalar.activation(out=xs[0:n, :], in_=xin[0:n, :], func=AF.Square,
                                                     accum_out=ssq[0:n, 0:1]), rt=[t_xin], wt=[t_xs, t_ssq])
            else:
                tb_i = pre_ss[1]
                op(DVE, lambda: nc.vector.reduce_sum(out=ssq[0:n, 0:1], in_=pre_ss[0][0:n, tb_i(col0), :], axis=AX.X),
                   rt=[pre_ss[2]], wt=[t_ssq])
            op(ACT, lambda: nc.scalar.activation(out=rstd[0:n, 0:1], in_=ssq[0:n, 0:1], func=AF.Sqrt,
                                                 scale=1.0 / D, bias=eps_t[0:n, 0:1]), rt=[t_ssq, t_eps], wt=[t_rstd])
            op(DVE, lambda: nc.vector.reciprocal(out=rstd[0:n, 0:1], in_=rstd[0:n, 0:1]), rt=[t_rstd], wt=[t_rstd])
            op(DVE, lambda: nc.vector.tensor_scalar(out=xs[0:n, :], in0=xin[0:n, :], scalar1=rstd[0:n, 0:1],
                                                    scalar2=None, op0=ALU.mult), rt=[t_xin, t_rstd], wt=[t_xs])
            px, t_px = ps_x[0]
            for q in range(8):
                for i in range(4):
                    kc = 4 * q + i
                    op(PE, lambda kc=kc, i=i: nc.tensor.transpose(out=px[:, i, 0:n], in_=xs[0:n, kc * 128:(kc + 1) * 128],
                                                                  identity=ident_b[0:n, 0:n]),
                       rt=[t_xs, t_idb], wt=[t_px], signal=(i == 3))
                op(DVE, lambda q=q: nc.vector.tensor_tensor(
                    out=dstT[:, 4 * q:4 * q + 4, col0:col0 + n], in0=px[:, :, 0:n],
                    in1=gtab[:, 4 * q:4 * q + 4].unsqueeze(2).to_broadcast([128, 4, n]), op=ALU.mult),
                   rt=[t_px, t_g], wt=[t_dst])

    hT, t_hT = L.push([128, KC, T], BF16)
    wslots = Slots(4, [128, KC, 128])
    mRh = R.mark()
    hTh, t_hTh = R.push([128, KC, TH], BF16)
    mR = R.mark()
    xin, t_xin = R.push([128, D], F32)
    xs, t_xs = R.push([128, D], BF16)
    ssq, t_ssq = R.push([128, 1], F32)
    rstd, t_rstd = R.push([128, 1], F32)
    blocks_h = [(x_halo[i * 128:(i + 1) * 128, :], 128, i * 128, None) for i in range(8)]
    blocks_o = [(x_own[i * 128:(i + 1) * 128, :], 128, i * 128, None) for i in range(8)] + [(x_smp, NS, TP, None)]
    norm_T(blocks_h, gmix, t_gm, hTh, t_hTh, xin, t_xin, xs, t_xs, ssq, t_ssq, rstd, t_rstd)
    norm_T(blocks_o, gmix, t_gm, hT, t_hT, xin, t_xin, xs, t_xs, ssq, t_ssq, rstd, t_rstd)
    R.pop_to(mR)
    if stop == 1:
        finalize(fw)
        return nc

    def qkv_col(qkv, g, h):
        return ((qkv * 3 + g) * 8 + h) * 128

    kvst = [R.push([128, 2052], BF16) for _ in range(2)]
    st32 = [R.push([128, T], F32) for _ in range(2)]
    tokst = [R.push([128, 8, 128], F32) for _ in range(2)]
    p2n = [0]

    def kv_job(kv, g, h):
        ci = (kv - 1) * 24 + g * 8 + h
        hl = HALO[g]

        def load():
            return wslots.load(wcols(w_in, qkv_col(kv, g, h), 128))

        def compute(k):
            wt_tile, wt_tok = wslots.use(k)
            i2 = p2n[0] % 2
            p2n[0] += 1
            kst, t_kst = kvst[i2]
            s32, t_s32 = st32[i2]
            tks, t_tks = tokst[i2]
            movs = []
            dests = []
            for p0 in range(0, hl, 512):
                n = min(512, hl - p0)
                movs.append((lambda kc, a=TH - hl + p0, n=n: hTh[:, kc, a:a + n], n))
                dests.append((p0, None))
            for (c0, n) in OWN_PIECES:
                movs.append((lambda kc, c0=c0, n=n: hT[:, kc, c0:c0 + n], n))
                dests.append((hl + c0, c0))

            def ep(pi, bank, t_bank, n):
                d0, c0 = dests[pi]
                op(ACT, lambda: nc.scalar.copy(out=kst[:, d0:d0 + n], in_=bank[:, 0:n]), rt=[t_bank], wt=[t_kst])
                if c0 is not None:
                    op(DVE, lambda: nc.vector.tensor_copy(out=s32[:, c0:c0 + n], in_=bank[:, 0:n]), rt=[t_bank], wt=[t_s32])

            gemm(wt_tile, wt_tok, KC, movs, ep, [t_hT, t_hTh])
            import os
            if not os.environ.get('NOKVD'):
                dma(SP, kvT_d[ci, :, 0:hl + T], kst[:, 0:hl + T], rt=[t_kst], wt=[t_kvd[ci]], st=t_kst)
            blks = list(range(8 - NBLK[g], 8)) if not os.environ.get('NOTR') else []
            for bi, blk in enumerate(blks):
                pt, t_pt = ps_o[bi % 2]
                op(PE, lambda blk=blk: nc.tensor.transpose(out=pt[:, :], in_=s32[:, blk * 128:(blk + 1) * 128],
                                                           identity=ident_f[:, :]), rt=[t_s32, t_idf], wt=[t_pt])
                if bi % 2 == 0:
                    op(ACT, lambda bi=bi: nc.scalar.copy(out=tks[:, bi, :], in_=pt[:, :]), rt=[t_pt], wt=[t_tks])
                else:
                    op(DVE, lambda bi=bi: nc.vector.tensor_copy(out=tks[:, bi, :], in_=pt[:, :]), rt=[t_pt], wt=[t_tks])
            dst = (kco if kv == 1 else vco)[g]
            if blks:
              dma(SP, dst[:, h * 128:(h + 1) * 128].rearrange("(b p) d -> p b d", p=128), tks[:, 0:len(blks), :],
                rt=[t_tks], wt=[t_out], st=t_tks)
            kn, t_kn = knew[g][kv - 1]
            if not os.environ.get('NOKN'):
                op(DVE, lambda: nc.vector.tensor_copy(out=kn[:, h, :], in_=s32[:, TP:T]), rt=[t_s32], wt=[t_kn])

        return Job(load, compute)

    def uhalo_job(i):
        def load():
            return wslots.load(wcols(w_in, 9216 + i * 128, 128))

        def compute(k):
            wt_tile, wt_tok = wslots.use(k)

            def ep(pi, bank, t_bank, n):
                op(DVE, lambda: nc.vector.tensor_copy(out=uhalo[:, i, :], in_=bank[:, 0:16]), rt=[t_bank], wt=[t_uh])

            gemm(wt_tile, wt_tok, KC, [(lambda kc: hTh[:, kc, TH - 16:TH], 16)], ep, [t_hTh])

        return Job(load, compute)

    jobs2 = [kv_job(kv, g, h) for kv in (1, 2) for g in range(3) for h in range(8)] + [uhalo_job(i) for i in range(16)]

    P3 = {}

    def p3_begin(_k=None):
        R.pop_to(mRh)
        P3["attnT"] = L.push([128, 8, T], BF16)
        P3["QT"] = [R.push([128, T], BF16) for _ in range(2)]
        P3["KT"] = R.push([128, 2052], BF16)
        P3["VT"] = R.push([128, 2052], BF16)
        P3["Ksm"] = [[R.push([128, NS, 128], BF16) for g in range(3)] for _ in range(2)]
        P3["Vsm"] = [[R.push([128, NS, 128], BF16) for g in range(3)] for _ in range(2)]
        P3["KTs"] = [R.push([128, 132], BF16) for _ in range(4)]
        P3["S"] = [R.push([128, 256], F32) for _ in range(4)]
        P3["P"] = [R.push([128, 256], BF16) for _ in range(4)]
        P3["PT"] = [R.push([128, 2, 128], BF16) for _ in range(4)]
        P3["Vb"] = [R.push([128, 2, 128], BF16) for _ in range(4)]
        P3["st"] = [[R.push([128, 1], F32) for _ in range(6)] for _ in range(4)]
        P3["bcl"] = [R.push([128, 128], F32) for _ in range(4)]
        P3["bcr"] = [R.push([128, 128], F32) for _ in range(4)]
        P3["rd"] = [R.push([128, 128], F32) for _ in range(4)]
        P3["tmp"] = [R.push([128, 2], F32) for _ in range(4)]
        P3["og"] = [R.push([128, T], F32) for _ in range(3)]
        P3["lg"] = [R.push([128, T], F32) for _ in range(3)]
        P3["Lm"] = R.push([128, T], F32)
        P3["Wm"] = R.push([128, T], F32)
        P3["acc"] = R.push([128, T], F32)
        P3["un"] = 0

    def attn_unit(g, h, QTt, t_QT, q_sl, nq, KTsrc, t_KTsrc, k_sl, nk, vsrcs, Dtab, t_D, d_rows, out_sl, sample=None):
        return dict(g=g, h=h, QTt=QTt, t_QT=t_QT, q_sl=q_sl, nq=nq, KTsrc=KTsrc, t_KTsrc=t_KTsrc, k_sl=k_sl, nk=nk,
                    vsrcs=vsrcs, Dtab=Dtab, t_D=t_D, d_rows=d_rows, out_sl=out_sl, sample=sample)

    def _ctx(U):
        i = U["idx"]
        u2, u4 = i % 2, i % 4
        C = dict(U)
        C["c"] = -SLOPES[U["g"]][U["h"]] * DIL[U["g"]]
        C["pss"], C["t_X"] = ps_s[u2]
        C["pst"] = ps_t[u2][0]
        C["psv"] = ps_v[u2][0]
        C["pso"], C["t_Y"] = ps_o[u2]
        C["psl"] = ps_l[u2][0]
        C["psr"] = ps_r[u2][0]
        C["psp"] = ps_p[u2][0]
        for nm in ("S", "P", "PT", "Vb", "bcl", "bcr", "rd", "tmp"):
            C[nm], C["t_" + nm] = P3[nm][u4]
        C["st"] = P3["st"][u4]
        C["og"], C["t_og"] = P3["og"][U["g"]]
        C["lg"], C["t_lg"] = P3["lg"][U["g"]]
        C["nkb"] = len(U["vsrcs"])
        C["vT"] = (U["vsrcs"][0][0] == "T")
        return C

    def st_A(C):
        nkb, nq, nk = C["nkb"], C["nq"], C["nk"]
        if C["vT"]:
            for kb in range(nkb):
                op(PE, lambda kb=kb: nc.tensor.transpose(out=C["psv"][:, kb * 128:(kb + 1) * 128], in_=C["vsrcs"][kb][1],
                                                         identity=ident_b[:, :]), rt=[C["vsrcs"][kb][2], t_idb], wt=[C["t_X"]],
                   signal=False)
        op(PE, lambda: nc.tensor.matmul(C["pss"][0:nq, 0:nk], lhsT=C["QTt"][:, C["q_sl"]], rhs=C["KTsrc"][:, C["k_sl"]],
                                        start=True, stop=True), rt=[C["t_QT"], C["t_KTsrc"]], wt=[C["t_X"]])

    def st_B(C):
        nkb, nq, nk = C["nkb"], C["nq"], C["nk"]
        (m, t_m), (negm, t_negm), (den, t_den) = C["st"][0:3]
        S, Pm = C["S"], C["P"]
        if C["vT"]:
            op(ACT, lambda: nc.scalar.copy(out=C["Vb"][:, 0:nkb, :],
                                           in_=C["psv"][:, 0:nkb * 128].rearrange("p (b q) -> p b q", q=128)),
               rt=[C["t_X"]], wt=[C["t_Vb"]])
        op(DVE, lambda: nc.vector.scalar_tensor_tensor(out=S[0:nq, 0:nk], in0=C["Dtab"][C["d_rows"], 0:nk], scalar=C["c"],
                                                       in1=C["pss"][0:nq, 0:nk], op0=ALU.mult, op1=ALU.add),
           rt=[C["t_X"], C["t_D"]], wt=[C["t_S"]])
        op(DVE, lambda: nc.vector.reduce_max(out=m[0:nq, :], in_=S[0:nq, 0:nk], axis=AX.X), rt=[C["t_S"]], wt=[t_m])
        op(DVE, lambda: nc.vector.tensor_scalar(out=negm[0:nq, :], in0=m[0:nq, :], scalar1=-1.0, scalar2=None,
                                                op0=ALU.mult), rt=[t_m], wt=[t_negm])
        op(ACT, lambda: nc.scalar.activation(out=Pm[0:nq, 0:nk], in_=S[0:nq, 0:nk], func=AF.Exp, bias=negm[0:nq, :],
                                             scale=1.0, accum_out=den[0:nq, :]), rt=[C["t_S"], t_negm], wt=[C["t_P"], t_den])

    def st_C(C):
        nkb, nq = C["nkb"], C["nq"]
        for kb in range(nkb):
            op(PE, lambda kb=kb: nc.tensor.transpose(out=C["pst"][:, kb * 128:kb * 128 + nq],
                                                     in_=C["P"][0:nq, kb * 128:(kb + 1) * 128],
                                                     identity=ident_b[0:nq, 0:nq]), rt=[C["t_P"], t_idb], wt=[C["t_X"]],
               signal=(kb == nkb - 1))

    def st_D(C):
        nkb, nq = C["nkb"], C["nq"]
        (m, t_m), (negm, t_negm), (den, t_den), (lnd, t_lnd), (lse, t_lse), (rden, t_rden) = C["st"]
        op(ACT, lambda: nc.scalar.copy(out=C["PT"][:, 0:nkb, 0:nq],
                                       in_=C["pst"][:, 0:nkb * 128].rearrange("p (b q) -> p b q", q=128)[:, :, 0:nq]),
           rt=[C["t_X"]], wt=[C["t_PT"]])
        op(ACT, lambda: nc.scalar.activation(out=lnd[0:nq, :], in_=den[0:nq, :], func=AF.Ln), rt=[t_den], wt=[t_lnd])
        op(DVE, lambda: nc.vector.tensor_tensor(out=lse[0:nq, :], in0=lnd[0:nq, :], in1=m[0:nq, :], op=ALU.add),
           rt=[t_lnd, t_m], wt=[t_lse])
        op(DVE, lambda: nc.vector.reciprocal(out=rden[0:nq, :], in_=den[0:nq, :]), rt=[t_den], wt=[t_rden])
        op(DVE, lambda: nc.vector.tensor_scalar(out=C["bcl"][0:nq, :], in0=ones_f[0:nq, :], scalar1=lse[0:nq, 0:1], scalar2=None,
                                                op0=ALU.mult), rt=[t_lse, t_1f], wt=[C["t_bcl"]])
        op(DVE, lambda: nc.vector.tensor_scalar(out=C["bcr"][0:nq, :], in0=ones_f[0:nq, :], scalar1=rden[0:nq, 0:1], scalar2=None,
                                                op0=ALU.mult), rt=[t_rden, t_1f], wt=[C["t_bcr"]])

    def st_E(C):
        nkb, nq = C["nkb"], C["nq"]
        if C["vT"]:
            vmats = [(C["Vb"][:, kb, :], C["t_Vb"]) for kb in range(nkb)]
        else:
            vmats = [(C["vsrcs"][kb][1], C["vsrcs"][kb][2]) for kb in range(nkb)]
        for kb in range(nkb):
            op(PE, lambda kb=kb: nc.tensor.matmul(C["pso"][:, 0:nq], lhsT=vmats[kb][0], rhs=C["PT"][:, kb, 0:nq],
                                                  start=(kb == 0), stop=(kb == nkb - 1)),
               rt=[vmats[kb][1], C["t_PT"]], wt=[C["t_Y"]], signal=False)
        if C["sample"] is not None:
            op(PE, lambda: nc.tensor.matmul(C["psp"][:, 0:1], lhsT=ones_b[0:1, :], rhs=C["P"][0:1, 128:129], start=True, stop=True),
               rt=[C["t_P"], t_1b], wt=[C["t_Y"]], signal=False)
        op(PE, lambda: nc.tensor.matmul(C["psl"][:, 0:nq], lhsT=C["bcl"][0:nq, :], rhs=ident_f[0:nq, 0:nq], start=True, stop=True),
           rt=[C["t_bcl"], t_idf], wt=[C["t_Y"]], signal=False)
        op(PE, lambda: nc.tensor.matmul(C["psr"][:, 0:nq], lhsT=C["bcr"][0:nq, :], rhs=ident_f[0:nq, 0:nq], start=True, stop=True),
           rt=[C["t_bcr"], t_idf], wt=[C["t_Y"]])

    def st_F(C):
        nq, out_sl = C["nq"], C["out_sl"]
        og, lg, rd = C["og"], C["lg"], C["rd"]
        op(ACT, lambda: nc.scalar.copy(out=rd[:, 0:nq], in_=C["psr"][:, 0:nq]), rt=[C["t_Y"]], wt=[C["t_rd"]])
        op(ACT, lambda: nc.scalar.copy(out=lg[:, out_sl], in_=C["psl"][:, 0:nq]), rt=[C["t_Y"]], wt=[C["t_lg"]])
        if C["sample"] is None:
            op(DVE, lambda: nc.vector.tensor_tensor(out=og[:, out_sl], in0=C["pso"][:, 0:nq], in1=rd[:, 0:nq], op=ALU.mult),
               rt=[C["t_Y"], C["t_rd"]], wt=[C["t_og"]])
        else:
            vnew_ap, t_vnew = C["sample"]
            tmp, t_tmp = C["tmp"], C["t_tmp"]
            op(DVE, lambda: nc.vector.tensor_tensor(out=tmp[:, 0:1], in0=C["psp"][:, 0:1], in1=vnew_ap, op=ALU.mult),
               rt=[C["t_Y"], t_vnew], wt=[t_tmp])
            op(DVE, lambda: nc.vector.tensor_tensor(out=tmp[:, 1:2], in0=C["pso"][:, 0:1], in1=tmp[:, 0:1], op=ALU.add),
               rt=[C["t_Y"], t_tmp], wt=[t_tmp])
            op(DVE, lambda: nc.vector.tensor_tensor(out=og[:, out_sl], in0=tmp[:, 1:2], in1=rd[:, 0:1], op=ALU.mult),
               rt=[t_tmp, C["t_rd"]], wt=[C["t_og"]])

    def run_units(units):
        cs = []
        for U in units:
            U["idx"] = P3["un"]
            P3["un"] += 1
            cs.append(_ctx(U))
        n = len(cs)
        for it in range(n + 2):
            if 0 <= it - 2 < n:
                st_E(cs[it - 2])
            if 0 <= it - 1 < n:
                st_C(cs[it - 1])
            if it < n:
                st_A(cs[it])
            if 0 <= it - 2 < n:
                st_F(cs[it - 2])
            if 0 <= it - 1 < n:
                st_D(cs[it - 1])
            if it < n:
                st_B(cs[it])

    def q_job(j, g, first=False):
        def load():
            return wslots.load(wcols(w_in, qkv_col(0, g, j), 128))

        def compute(k):
            if first:
                p3_begin()
            wt_tile, wt_tok = wslots.use(k)
            hl = HALO[g]
            KT, t_KT = P3["KT"]
            VT, t_VT = P3["VT"]
            QT, t_QT = P3["QT"][(j * 3 + g) % 2]
            dma(SP, KT[:, 0:hl + T], kvT_d[g * 8 + j, :, 0:hl + T], rt=[t_kvd[g * 8 + j]], wt=[t_KT])
            dma(SP, VT[:, 0:hl + T], kvT_d[24 + g * 8 + j, :, 0:hl + T], rt=[t_kvd[24 + g * 8 + j]], wt=[t_VT])
            Ksm, t_Ksm = P3["Ksm"][j % 2][g]
            Vsm, t_Vsm = P3["Vsm"][j % 2][g]
            dl = DIL[g]
            ckv = ck[g].rearrange("b (r c) -> b r c", c=1024)
            cvv = cv[g].rearrange("b (r c) -> b r c", c=1024)
            dma(POOL, Ksm[:, :, :], ckv[:, 0:WIN[g]:dl, j * 128:(j + 1) * 128].rearrange("b r d -> r b d"), wt=[t_Ksm])
            dma(POOL, Vsm[:, :, :], cvv[:, 0:WIN[g]:dl, j * 128:(j + 1) * 128].rearrange("b r d -> r b d"), wt=[t_Vsm])

            def ep(pi, bank, t_bank, n):
                c0 = OWN_PIECES[pi][0]
                op(ACT, lambda: nc.scalar.activation(out=QT[:, c0:c0 + n], in_=bank[:, 0:n], func=AF.Copy, scale=SCALE),
                   rt=[t_bank], wt=[t_QT])

            gemm(wt_tile, wt_tok, KC, [(lambda kc, c0=c0, n=n: hT[:, kc, c0:c0 + n], n) for (c0, n) in OWN_PIECES], ep, [t_hT])
            for b in range(NS):
                KTs, t_KTs = P3["KTs"][b]
                psv, t_psv = ps_v[b % 2]
                op(PE, lambda: nc.tensor.transpose(out=psv[:, 0:128], in_=Ksm[:, b, :], identity=ident_b[:, :]),
                   rt=[t_Ksm, t_idb], wt=[t_psv])
                op(DVE, lambda: nc.vector.tensor_copy(out=KTs[:, 0:128], in_=psv[:, 0:128]), rt=[t_psv], wt=[t_KTs])
                op(DVE, lambda: nc.vector.tensor_copy(out=KTs[:, 128:129], in_=KT[:, hl + TP + b:hl + TP + b + 1]),
                   rt=[t_KT], wt=[t_KTs])
            units = []
            if g == 0:
                for i in range(8):
                    units.append(attn_unit(g, j, QT, t_QT, slice(128 * i, 128 * i + 128), 128, KT, t_KT, slice(128 * i, 128 * i + 256), 256,
                                           [("T", VT[:, 128 * i + 128 * kb:128 * i + 128 * kb + 128], t_VT) for kb in range(2)],
                                           Db if i == 0 else Da, t_Db if i == 0 else t_Da, slice(0, 128), slice(128 * i, 128 * i + 128)))
            elif g == 1:
                for r in range(4):
                    for blk in range(2):
                        b0 = r + 512 * blk
                        units.append(attn_unit(g, j, QT, t_QT, slice(b0, b0 + 512, 4), 128, KT, t_KT, slice(b0, b0 + 1024, 4), 256,
                                               [("T", VT[:, b0 + 512 * kb:b0 + 512 * kb + 512:4], t_VT) for kb in range(2)],
                                               Db if blk == 0 else Da, t_Db if blk == 0 else t_Da, slice(0, 128), slice(b0, b0 + 512, 4)))
            else:
                for r in range(16):
                    units.append(attn_unit(g, j, QT, t_QT, slice(r, 1024, 16), 64, KT, t_KT, slice(r, 2048, 16), 128,
                                           [("T", VT[:, r:2048:16], t_VT)], Dc, t_Dc, slice(0, 64), slice(r, 1024, 16)))
            for b in range(NS):
                KTs, t_KTs = P3["KTs"][b]
                units.append(attn_unit(g, j, QT, t_QT, slice(TP + b, TP + b + 1), 1, KTs, t_KTs, slice(0, 129), 129,
                                       [("N", Vsm[:, b, :], t_Vsm)], Ds, t_Ds, slice(0, 1), slice(TP + b, TP + b + 1),
                                       sample=(VT[:, hl + TP + b:hl + TP + b + 1], t_VT)))
            run_units(units)
            if g == 2:
                merge(j)

        return Job(load, compute)

    def merge(j):
        (o0, t_o0), (o1, t_o1), (o2, t_o2) = P3["og"]
        (l0, t_l0), (l1, t_l1), (l2, t_l2) = P3["lg"]
        Lm, t_L = P3["Lm"]
        Wm, t_W = P3["Wm"]
        acc, t_acc = P3["acc"]
        attnT, t_attnT = P3["attnT"]
        V = nc.vector
        op(DVE, lambda: V.tensor_tensor(out=Lm[:], in0=l0[:], in1=l1[:], op=ALU.max), rt=[t_l0, t_l1], wt=[t_L])
        op(DVE, lambda: V.tensor_tensor(out=Lm[:], in0=Lm[:], in1=l2[:], op=ALU.max), rt=[t_L, t_l2], wt=[t_L])
        for (l, t_l) in ((l0, t_l0), (l1, t_l1), (l2, t_l2)):
            op(DVE, lambda l=l: V.tensor_tensor(out=l[:], in0=l[:], in1=Lm[:], op=ALU.subtract), rt=[t_l, t_L], wt=[t_l])
            op(ACT, lambda l=l: nc.scalar.activation(out=l[:], in_=l[:], func=AF.Exp), rt=[t_l], wt=[t_l])
        op(DVE, lambda: V.tensor_tensor(out=Wm[:], in0=l0[:], in1=l1[:], op=ALU.add), rt=[t_l0, t_l1], wt=[t_W])
        op(DVE, lambda: V.tensor_tensor(out=Wm[:], in0=Wm[:], in1=l2[:], op=ALU.add), rt=[t_W, t_l2], wt=[t_W])
        op(DVE, lambda: V.reciprocal(out=Wm[:], in_=Wm[:]), rt=[t_W], wt=[t_W])
        op(DVE, lambda: V.tensor_tensor(out=acc[:], in0=o0[:], in1=l0[:], op=ALU.mult), rt=[t_o0, t_l0], wt=[t_acc])
        op(DVE, lambda: V.tensor_tensor(out=o1[:], in0=o1[:], in1=l1[:], op=ALU.mult), rt=[t_o1, t_l1], wt=[t_o1])
        op(DVE, lambda: V.tensor_tensor(out=acc[:], in0=acc[:], in1=o1[:], op=ALU.add), rt=[t_acc, t_o1], wt=[t_acc])
        op(DVE, lambda: V.tensor_tensor(out=o2[:], in0=o2[:], in1=l2[:], op=ALU.mult), rt=[t_o2, t_l2], wt=[t_o2])
        op(DVE, lambda: V.tensor_tensor(out=acc[:], in0=acc[:], in1=o2[:], op=ALU.add), rt=[t_acc, t_o2], wt=[t_acc])
        op(DVE, lambda: V.tensor_tensor(out=attnT[:, j, :], in0=acc[:], in1=Wm[:], op=ALU.mult), rt=[t_acc, t_W], wt=[t_attnT])

    jobs3 = [q_job(j, g, first=(j == 0 and g == 0)) for j in range(8) for g in range(3)]

    P4 = {}

    def p4_begin():
        attn_keep = P3["attnT"]
        R.pop_to(mRh)
        P4["attnT"] = attn_keep
        P4["wup"] = Slots(2, [128, 8, 128])
        P4["wpl"] = Slots(2, [128, 4, 128])
        P4["U0"] = R.push([128, 16 + TP], F32)
        P4["T1"] = R.push([128, 16 + TP], F32)
        P4["T2"] = R.push([128, 16 + TP], F32)
        P4["pooled"] = R.push([128, 4, T], BF16)
        P4["sgA"] = R.push([128, T], F32)
        P4["sgB"] = R.push([128, T], F32)
        P4["t1"] = R.push([128, T], F32)
        P4["t2"] = R.push([128, T], F32)
        P4["mixst"] = [R.push([128, T], BF16) for _ in range(2)]
        P4["hist"] = R.push([128, 16, NS, 16], F32)
        P4["sprow"] = R.push([NS * 15, 2048], F32)
        P4["uprow"] = R.push([15, 2048], F32)
        P4["usrow"] = R.push([NS, 2048], F32)
        P4["ssum"] = R.push([128, NS], F32)
        sprow, t_sp = P4["sprow"]
        hist, t_hist = P4["hist"]
        dma(SP, sprow[:, :], spool, wt=[t_sp])
        for chn in range(16):
            pt, t_pt = ps_o[chn % 2]
            op(PE, lambda chn=chn: nc.tensor.transpose(out=pt[:, 0:NS * 15], in_=sprow[:, chn * 128:(chn + 1) * 128],
                                                       identity=ident_f[0:NS * 15, 0:NS * 15]), rt=[t_sp, t_idf], wt=[t_pt])
            op(ACT, lambda chn=chn: nc.scalar.copy(out=hist[:, chn, :, 0:15],
                                                   in_=pt[:, 0:NS * 15].rearrange("p (b r) -> p b r", r=15)),
               rt=[t_pt], wt=[t_hist])

    PWIN = (2, 4, 8, 16)

    def u_job(pg, i, first=False):
        chn = pg * 4 + i

        def load():
            return wslots.load(wcols(w_in, 9216 + chn * 128, 128))

        def compute(k):
            if first:
                p4_begin()
            wt_tile, wt_tok = wslots.use(k)
            U0, t_U0 = P4["U0"]
            T1, t_T1 = P4["T1"]
            T2, t_T2 = P4["T2"]
            pooled, t_pooled = P4["pooled"]
            hist, t_hist = P4["hist"]
            uprow, t_upr = P4["uprow"]
            usrow, t_usr = P4["usrow"]
            ssum, t_ssum = P4["ssum"]
            w = PWIN[pg]
            V = nc.vector
            op(DVE, lambda: V.tensor_copy(out=U0[:, 0:16], in_=uhalo[:, chn, :]), rt=[t_uh], wt=[t_U0])

            def ep(pi, bank, t_bank, n):
                c0 = OWN_PIECES[pi][0]
                npr = min(n, TP - c0)
                op(ACT, lambda: nc.scalar.copy(out=U0[:, 16 + c0:16 + c0 + npr], in_=bank[:, 0:npr]), rt=[t_bank], wt=[t_U0])
                if npr < n:
                    op(DVE, lambda: V.tensor_copy(out=hist[:, chn, :, 15], in_=bank[:, npr:n]), rt=[t_bank], wt=[t_hist])

            gemm(wt_tile, wt_tok, KC, [(lambda kc, c0=c0, n=n: hT[:, kc, c0:c0 + n], n) for (c0, n) in OWN_PIECES], ep, [t_hT])
            NU = 16 + TP
            src, t_src = U0, t_U0
            sh = 1
            bufs = [(T1, t_T1), (T2, t_T2)]
            bi = 0
            while sh < w:
                dst, t_dst = bufs[bi % 2]
                bi += 1
                lo = 2 * sh - 1
                op(DVE, lambda src=src, dst=dst, sh=sh, lo=lo: V.tensor_tensor(out=dst[:, lo:NU], in0=src[:, lo:NU],
                                                                               in1=src[:, lo - sh:NU - sh], op=ALU.add),
                   rt=[t_src], wt=[t_dst])
                src, t_src = dst, t_dst
                sh *= 2
            tot, t_tot = src, t_src
            op(DVE, lambda: V.scalar_tensor_tensor(out=pooled[:, i, 16:TP], in0=tot[:, 32:NU], scalar=1.0 / w,
                                                   in1=U0[:, 32:NU], op0=ALU.mult, op1=ALU.subtract),
               rt=[t_tot, t_U0], wt=[t_pooled])
            ot, t_ot = bufs[bi % 2]
            op(DVE, lambda: V.tensor_tensor(out=ot[:, 0:16], in0=tot[:, 16:32], in1=invc[:, pg, :], op=ALU.mult),
               rt=[t_tot, t_invc], wt=[t_ot])
            op(DVE, lambda: V.tensor_tensor(out=pooled[:, i, 0:16], in0=ot[:, 0:16], in1=U0[:, 16:32], op=ALU.subtract),
               rt=[t_ot, t_U0], wt=[t_pooled])
            op(DVE, lambda: V.tensor_reduce(out=ssum[:, :], in_=hist[:, chn, :, 16 - w:16], axis=AX.X, op=ALU.add),
               rt=[t_hist], wt=[t_ssum])
            op(DVE, lambda: V.scalar_tensor_tensor(out=pooled[:, i, TP:T], in0=ssum[:, :], scalar=1.0 / w,
                                                   in1=hist[:, chn, :, 15], op0=ALU.mult, op1=ALU.subtract),
               rt=[t_ssum, t_hist], wt=[t_pooled])
            pt, t_pt = ps_o[0]
            op(PE, lambda: nc.tensor.transpose(out=pt[0:15, :], in_=U0[:, 16 + TP - 15:16 + TP], identity=ident_f[:, :]),
               rt=[t_U0, t_idf], wt=[t_pt])
            op(ACT, lambda: nc.scalar.copy(out=uprow[:, chn * 128:(chn + 1) * 128], in_=pt[0:15, :]), rt=[t_pt], wt=[t_upr])
            pt2, t_pt2 = ps_o[1]
            op(PE, lambda: nc.tensor.transpose(out=pt2[0:NS, :], in_=hist[:, chn, :, 15], identity=ident_f[:, :]),
               rt=[t_hist, t_idf], wt=[t_pt2])
            op(ACT, lambda: nc.scalar.copy(out=usrow[:, chn * 128:(chn + 1) * 128], in_=pt2[0:NS, :]), rt=[t_pt2], wt=[t_usr])
            if chn == 15:
                dma(SP, pool_p, uprow[:, :], rt=[t_upr], wt=[t_out], st=t_upr)
                dma(SP, pool_s.rearrange("b (r c) -> b r c", c=2048)[:, 14, :], usrow[:, :], rt=[t_usr], wt=[t_out], st=t_usr)

        return Job(load, compute)

    def gate_job(c, which):
        def load():
            return wslots.load(wcols(w_in, 11264 + which * 4096 + c * 128, 128))

        def compute(k):
            wt_tile, wt_tok = wslots.use(k)
            sg, t_sg = P4["sgA"] if which == 0 else P4["sgB"]

            def ep(pi, bank, t_bank, n):
                c0 = OWN_PIECES[pi][0]
                op(ACT, lambda: nc.scalar.activation(out=sg[:, c0:c0 + n], in_=bank[:, 0:n], func=AF.Sigmoid),
                   rt=[t_bank], wt=[t_sg])

            gemm(wt_tile, wt_tok, KC, [(lambda kc, c0=c0, n=n: hT[:, kc, c0:c0 + n], n) for (c0, n) in OWN_PIECES], ep, [t_hT])

        return Job(load, compute)

    def a_job(c):
        def load():
            return P4["wup"].load(wcols(w_up, c * 128, 128, nkc=8))

        def compute(k):
            wt_tile, wt_tok = P4["wup"].use(k)
            attnT, t_attnT = P4["attnT"]
            sgA, t_sgA = P4["sgA"]
            t1, t_t1 = P4["t1"]

            def ep(pi, bank, t_bank, n):
                c0 = OWN_PIECES[pi][0]
                op(DVE, lambda: nc.vector.tensor_tensor(out=t1[:, c0:c0 + n], in0=bank[:, 0:n], in1=sgA[:, c0:c0 + n], op=ALU.mult),
                   rt=[t_bank, t_sgA], wt=[t_t1])

            gemm(wt_tile, wt_tok, 8, [(lambda kc, c0=c0, n=n: attnT[:, kc, c0:c0 + n], n) for (c0, n) in OWN_PIECES], ep, [t_attnT])

        return Job(load, compute)

    def pb_job(c):
        pg = c // 8

        def load():
            return P4["wpl"].load(w_pool[pg, :, (c % 8) * 128:(c % 8) * 128 + 128].rearrange("(kc p) n -> p kc n", p=128))

        def compute(k):
            wt_tile, wt_tok = P4["wpl"].use(k)
            pooled, t_pooled = P4["pooled"]
            sgB, t_sgB = P4["sgB"]
            t1, t_t1 = P4["t1"]
            t2, t_t2 = P4["t2"]
            mixst, t_mixst = P4["mixst"][c % 2]

            def ep(pi, bank, t_bank, n):
                c0 = OWN_PIECES[pi][0]
                op(DVE, lambda: nc.vector.scalar_tensor_tensor(out=t2[:, c0:c0 + n], in0=bank[:, 0:n], scalar=pscale[:, c:c + 1],
                                                               in1=sgB[:, c0:c0 + n], op0=ALU.mult, op1=ALU.mult),
                   rt=[t_bank, t_sgB, t_psc], wt=[t_t2])
                op(DVE, lambda: nc.vector.tensor_tensor(out=mixst[:, c0:c0 + n], in0=t1[:, c0:c0 + n], in1=t2[:, c0:c0 + n], op=ALU.add),
                   rt=[t_t1, t_t2], wt=[t_mixst])

            gemm(wt_tile, wt_tok, 4, [(lambda kc, c0=c0, n=n: pooled[:, kc, c0:c0 + n], n) for (c0, n) in OWN_PIECES], ep, [t_pooled])
            dma(SP, mixT_d[c], mixst[:, :], rt=[t_mixst], wt=[t_mixd], st=t_mixst)

        return Job(load, compute)

    jobs4 = []
    for pg in range(4):
        for i in range(4):
            jobs4.append(u_job(pg, i, first=(pg == 0 and i == 0)))
        for c in range(pg * 8, pg * 8 + 8):
            jobs4 += [gate_job(c, 0), a_job(c), gate_job(c, 1), pb_job(c)]

    if stop == 2:
        import os
        _nj = int(os.environ.get('P2NJ', '999'))
        run_jobs((jobs2[:_nj] + ([] if os.environ.get('NOUH') else jobs2[-1:])) if _nj < 999 else jobs2)
        finalize(fw)
        return nc
    if stop == 3:
        run_jobs(jobs2 + jobs3)
        finalize(fw)
        return nc
    run_jobs(jobs2 + jobs3 + jobs4)
    if stop == 4:
        finalize(fw)
        return nc

    rows_st, t_rows = R.push([32, 128], F32)
    for g in range(3):
        for kv in range(2):
            kn, t_kn = knew[g][kv]
            pt, t_pt = ps_o[(g * 2 + kv) % 2]
            op(PE, lambda: nc.tensor.transpose(out=pt[0:32, :], in_=kn[:, :, :].rearrange("p h b -> p (h b)"), identity=ident_f[:, :]),
               rt=[t_kn, t_idf], wt=[t_pt])
            op(ACT, lambda: nc.scalar.copy(out=rows_st[:, :], in_=pt[0:32, :]), rt=[t_pt], wt=[t_rows])
            dst = (kso if kv == 0 else vso)[g].rearrange("b (r h d) -> r h b d", h=8, d=128)[WIN[g] - 1]
            for h in range(8):
                dma(SP, dst[h], rows_st[4 * h:4 * h + 4, :], rt=[t_rows], wt=[t_out], st=t_rows)

    R.pop_to(0)
    L.pop_to(L.mark() - 2)
    mL = L.mark()
    mixT, t_mixT = L.push([128, KC, T], BF16)
    dma(SP, mixT[:, :, :], mixT_d.rearrange("c p t -> p c t"), rt=[t_mixd], wt=[t_mixT])
    woslots = Slots(2, [128, KC, 512])
    xsl = [R.push([128, 512], F32) for _ in range(2)]
    x1st = [R.push([128, 512], F32) for _ in range(2)]
    sqj, t_sqj = R.push([128, 512], BF16)
    p5n = [0]

    def rows_of(dr_own, dr_smp, c0, n):
        return dr_own[c0:c0 + n, :] if c0 < TP else dr_smp[0:n, :]

    def wo_job(nn):
        def load():
            return woslots.load(wcols(w_out, nn * 512, 512))

        def compute(k):
            wt_tile, wt_tok = woslots.use(k)
            for tb, (c0, n) in enumerate(TBLK):
                i2 = p5n[0] % 2
                p5n[0] += 1
                xs_, t_xs_ = xsl[i2]
                x1_, t_x1_ = x1st[i2]
                dma(SP, xs_[0:n, :], rows_of(x_own, x_smp, c0, n)[:, nn * 512:(nn + 1) * 512], wt=[t_xs_])
                bank, t_bank = next_bank()
                for kc in range(KC):
                    op(PE, lambda kc=kc: nc.tensor.matmul(bank[0:n, :], lhsT=mixT[:, kc, c0:c0 + n], rhs=wt_tile[:, kc, :],
                                                          start=(kc == 0), stop=(kc == KC - 1)),
                       rt=[wt_tok, t_mixT], wt=[t_bank], signal=(kc == KC - 1))
                op(DVE, lambda: nc.vector.tensor_tensor(out=x1_[0:n, :], in0=bank[0:n, :], in1=xs_[0:n, :], op=ALU.add),
                   rt=[t_bank, t_xs_], wt=[t_x1_])
                op(ACT, lambda: nc.scalar.activation(out=sqj[0:n, :], in_=x1_[0:n, :], func=AF.Square,
                                                     accum_out=ss1[0:n, tb, nn:nn + 1]), rt=[t_x1_], wt=[t_sqj, t_ss1])
                dma(SP, xa_d[c0:c0 + n, nn * 512:(nn + 1) * 512], x1_[0:n, :], rt=[t_x1_], wt=[t_xa], st=t_x1_)

        return Job(load, compute)

    run_jobs([wo_job(nn) for nn in range(8)], pf=1)
    if stop == 5:
        finalize(fw)
        return nc

    R.pop_to(0)
    L.pop_to(mL)
    hfT, t_hfT = L.push([128, KC, T], BF16)
    xin, t_xin = R.push([128, D], F32)
    xs, t_xs = R.push([128, D], BF16)
    ssq, t_ssq = R.push([128, 1], F32)
    rstd, t_rstd = R.push([128, 1], F32)
    blocks_x1 = [(xa_d[c0:c0 + n, :], n, c0, t_xa) for (c0, n) in TBLK]
    norm_T(blocks_x1, gffn, t_gf, hfT, t_hfT, xin, t_xin, xs, t_xs, ssq, t_ssq, rstd, t_rstd,
           pre_ss=(ss1, lambda col0: col0 // 128, t_ss1))
    R.pop_to(0)
    if stop == 6:
        finalize(fw)
        return nc

    ffT, t_ffT = L.push([128, 29, T], BF16)
    scr = [(xa_d, t_xa), (xb_d, t_xb)]
    for ti, (cc0, nch) in enumerate(THIRDS):
        mR6 = R.mark()
        gslots = Slots(4, [128, KC, 128])
        sgs = [R.push([128, T], F32) for _ in range(2)]

        def gu_job(c, which):
            def load():
                return gslots.load(wcols(w_gate if which == 0 else w_upf, c * 128, 128))

            def compute(k):
                wt_tile, wt_tok = gslots.use(k)
                sg, t_sg = sgs[c % 2]

                def ep(pi, bank, t_bank, n):
                    c0 = OWN_PIECES[pi][0]
                    if which == 0:
                        op(ACT, lambda: nc.scalar.activation(out=sg[:, c0:c0 + n], in_=bank[:, 0:n], func=AF.Silu),
                           rt=[t_bank], wt=[t_sg])
                    else:
                        op(DVE, lambda: nc.vector.tensor_tensor(out=ffT[:, c - cc0, c0:c0 + n], in0=bank[:, 0:n],
                                                                in1=sg[:, c0:c0 + n], op=ALU.mult),
                           rt=[t_bank, t_sg], wt=[t_ffT])

                gemm(wt_tile, wt_tok, KC, [(lambda kc, c0=c0, n=n: hfT[:, kc, c0:c0 + n], n) for (c0, n) in OWN_PIECES], ep, [t_hfT])

            return Job(load, compute)

        run_jobs([gu_job(c, which) for c in range(cc0, cc0 + nch) for which in (0, 1)])
        R.pop_to(mR6)
        wdslots = Slots(2, [128, 29, 512])
        psl_ = [R.push([128, 512], F32) for _ in range(2)]
        ost_ = [R.push([128, 512], F32) for _ in range(2)]
        sqj, t_sqj = R.push([128, 512], BF16)
        (src_d, t_src), (dst_d, t_dst) = scr[ti % 2], scr[(ti + 1) % 2]
        p6n = [0]
        last = (ti == 2)

        def wd_job(nn):
            def load():
                t, tk = wdslots.tiles[wdslots.i % 2]
                k = wdslots.i % 2
                wdslots.i += 1
                assert not wdslots.pending[k]
                wdslots.pending[k] = True
                dma(POOL, t[:, 0:nch, :], wcols(w_down, nn * 512, 512, r0=cc0 * 128, nkc=nch), wt=[tk])
                return k

            def compute(k):
                wt_tile, wt_tok = wdslots.use(k)
                for tb, (c0, n) in enumerate(TBLK):
                    i2 = p6n[0] % 2
                    p6n[0] += 1
                    pv_, t_pv_ = psl_[i2]
                    o_, t_o_ = ost_[i2]
                    dma(SP, pv_[0:n, :], src_d[c0:c0 + n, nn * 512:(nn + 1) * 512], rt=[t_src], wt=[t_pv_])
                    bank, t_bank = next_bank()
                    for kc in range(nch):
                        op(PE, lambda kc=kc: nc.tensor.matmul(bank[0:n, :], lhsT=ffT[:, kc, c0:c0 + n], rhs=wt_tile[:, kc, :],
                                                              start=(kc == 0), stop=(kc == nch - 1)),
                           rt=[wt_tok, t_ffT], wt=[t_bank], signal=(kc == nch - 1))
                    op(DVE, lambda: nc.vector.tensor_tensor(out=o_[0:n, :], in0=bank[0:n, :], in1=pv_[0:n, :], op=ALU.add),
                       rt=[t_bank, t_pv_], wt=[t_o_])
                    if last:
                        op(ACT, lambda: nc.scalar.activation(out=sqj[0:n, :], in_=o_[0:n, :], func=AF.Square,
                                                             accum_out=ss3[0:n, tb, nn:nn + 1]), rt=[t_o_], wt=[t_sqj, t_ss3])
                    dma(SP, dst_d[c0:c0 + n, nn * 512:(nn + 1) * 512], o_[0:n, :], rt=[t_o_], wt=[t_dst], st=t_o_)

            return Job(load, compute)

        run_jobs([wd_job(nn) for nn in range(8)], pf=1)
        R.pop_to(mR6)

    R.pop_to(0)
    L.pop_to(mL)
    fin_d, t_fin = scr[1]
    gfin, t_gfin = L.push([128, D], F32)
    dma(SP, gfin[:, :], g_fin.partition_broadcast(128), wt=[t_gfin])
    xin7 = [R.push([128, D], F32) for _ in range(2)]
    yst7 = [R.push([128, D], F32) for _ in range(2)]
    ssq, t_ssq = R.push([128, 1], F32)
    rstd, t_rstd = R.push([128, 1], F32)
    for tb, (c0, n) in enumerate(TBLK):
        xi, t_xi = xin7[tb % 2]
        yo, t_yo = yst7[tb % 2]
        dma(SP, xi[0:n, :], fin_d[c0:c0 + n, :], rt=[t_fin], wt=[t_xi])
        op(DVE, lambda: nc.vector.reduce_sum(out=ssq[0:n, 0:1], in_=ss3[0:n, tb, :], axis=AX.X), rt=[t_ss3], wt=[t_ssq])
        op(ACT, lambda: nc.scalar.activation(out=rstd[0:n, 0:1], in_=ssq[0:n, 0:1], func=AF.Sqrt, scale=1.0 / D,
                                             bias=eps_t[0:n, 0:1]), rt=[t_ssq, t_eps], wt=[t_rstd])
        op(DVE, lambda: nc.vector.reciprocal(out=rstd[0:n, 0:1], in_=rstd[0:n, 0:1]), rt=[t_rstd], wt=[t_rstd])
        op(DVE, lambda: nc.vector.scalar_tensor_tensor(out=yo[0:n, :], in0=xi[0:n, :], scalar=rstd[0:n, 0:1], in1=gfin[0:n, :],
                                                       op0=ALU.mult, op1=ALU.mult), rt=[t_xi, t_rstd, t_gfin], wt=[t_yo])
        dma(SP, rows_of(y_own, y_smp, c0, n), yo[0:n, :], rt=[t_yo], wt=[t_out], st=t_yo)
    finalize(fw)
    return nc


def _tables(first_half):
    qi = np.arange(128)[:, None]
    kj = np.arange(256)[None, :]
    delta = qi + 128 - kj
    valid = (delta >= 0) & (delta <= 128)
    Da = np.where(valid, delta, BIG).astype(np.float32)
    vb = valid & ~((kj < 128) & first_half)
    Db = np.where(vb, delta, BIG).astype(np.float32)
    qi = np.arange(64)[:, None]
    kj = np.arange(128)[None, :]
    delta = qi + 64 - kj
    vc = (delta >= 0) & ~((kj < 64) & first_half)
    Dc = np.where(vc, delta, BIG).astype(np.float32)
    Ds = np.concatenate([128 - np.arange(128), [0]]).astype(np.float32)[None, :]
    invc = np.zeros((128, 4, 16), np.float32)
    for g, w in enumerate((2, 4, 8, 16)):
        pos = np.arange(16)
        cnt = np.minimum(w, pos + 1) if first_half else np.full(16, w)
        invc[:, g, :] = (1.0 / cnt)[None, :]
    return Da, Db, Dc, Ds, invc


_NC_CACHE = {}


def _in_maps(x_prompt, x_sample, cache_k_w128, cache_v_w128, cache_k_w512, cache_v_w512, cache_k_w2048, cache_v_w2048,
             state_pool, norm_mix, w_in, w_up_attn, w_pool_map, pool_scale, w_out, norm_ffn, w_ffn_gate, w_ffn_up,
             w_ffn_down, norm_final, clist):
    f = lambda a: np.ascontiguousarray(np.asarray(a, dtype=np.float32))
    x_prompt = f(x_prompt); x_sample = f(x_sample)
    cks = [f(cache_k_w128)[0], f(cache_k_w512)[0], f(cache_k_w2048)[0]]
    cvs = [f(cache_v_w128)[0], f(cache_v_w512)[0], f(cache_v_w2048)[0]]
    sp = f(state_pool)[0]
    lay = lambda v: np.ascontiguousarray(f(v).reshape(KC, 128).T)
    shared = {
        "w_in": f(w_in)[0], "w_up": f(w_up_attn)[0], "w_pool": f(w_pool_map)[0], "w_out": f(w_out)[0],
        "w_gate": f(w_ffn_gate)[0], "w_upf": f(w_ffn_up)[0], "w_down": f(w_ffn_down)[0],
        "g_mix": lay(norm_mix), "g_ffn": lay(norm_ffn), "pscale": lay(pool_scale),
        "g_fin": f(norm_final).reshape(1, D), "ident": np.eye(128, dtype=np.float32),
    }
    in_maps = []
    for c in clist:
        b, half = c // 2, c % 2
        t0 = half * TP
        m = dict(shared)
        m["x_own"] = np.ascontiguousarray(x_prompt[b, t0:t0 + TP])
        m["x_smp"] = np.ascontiguousarray(x_sample[NS * c:NS * c + NS, 0])
        m["x_halo"] = np.ascontiguousarray(x_prompt[b, 0:TH]) if half else np.zeros((TH, D), np.float32)
        for g in range(3):
            m["ck%d" % g] = np.ascontiguousarray(cks[g][NS * c:NS * c + NS].reshape(NS, -1))
            m["cv%d" % g] = np.ascontiguousarray(cvs[g][NS * c:NS * c + NS].reshape(NS, -1))
        m["spool"] = np.ascontiguousarray(sp[NS * c:NS * c + NS].reshape(NS * 15, 2048))
        Da, Db, Dc, Ds, invc = _tables(half == 0)
        m.update(Da=Da, Db=Db, Dc=Dc, Ds=Ds, invc=invc)
        in_maps.append(m)
    return in_maps


def kernel(x_prompt, x_sample, cache_k_w128, cache_v_w128, cache_k_w512, cache_v_w512, cache_k_w2048, cache_v_w2048,
           state_pool, norm_mix, w_in, w_up_attn, w_pool_map, pool_scale, w_out, norm_ffn, w_ffn_gate, w_ffn_up,
           w_ffn_down, norm_final, _debug=False, _cores=8, _clist=None, _stop=99):
    in_maps = _in_maps(x_prompt, x_sample, cache_k_w128, cache_v_w128, cache_k_w512, cache_v_w512, cache_k_w2048,
                       cache_v_w2048, state_pool, norm_mix, w_in, w_up_attn, w_pool_map, pool_scale, w_out, norm_ffn,
                       w_ffn_gate, w_ffn_up, w_ffn_down, norm_final, _clist if _clist is not None else list(range(_cores)))
    if _stop < 5:
        for m in in_maps:
            for k in ("w_out", "w_gate", "w_upf", "w_down"):
                m.pop(k)
    key = (bool(_debug), _stop)
    if key not in _NC_CACHE:
        _NC_CACHE[key] = build(debug=_debug, stop=_stop)
    nc = _NC_CACHE[key]
    res = run_bass_kernel_spmd(nc, in_maps, core_ids=list(range(len(in_maps))))
    r = res.results
    if _debug:
        return r
    B = 4
    y_prompt = np.stack([np.concatenate([r[2 * b]["y_own"], r[2 * b + 1]["y_own"]], 0) for b in range(B)])
    y_sample = np.concatenate([r[c]["y_smp"] for c in range(8)], 0)[:, None, :]
    outs = [y_prompt, y_sample]
    for g in range(3):
        for nm in ("kc", "vc"):
            if g < 2:
                a = np.stack([r[2 * b + 1]["%s%d" % (nm, g)] for b in range(B)])
            else:
                a = np.stack([np.concatenate([r[2 * b]["%s%d" % (nm, g)], r[2 * b + 1]["%s%d" % (nm, g)]], 0) for b in range(B)])
            outs.append(a.reshape(1, B, -1, 8, 128))
    outs.append(np.stack([r[2 * b + 1]["pool_p"] for b in range(B)])[None])
    for g in range(3):
        for nm in ("ks", "vs"):
            a = np.concatenate([r[c]["%s%d" % (nm, g)] for c in range(8)], 0)
            outs.append(a.reshape(1, 32, WIN[g], 8, 128))
    outs.append(np.concatenate([r[c]["pool_s"] for c in range(8)], 0).reshape(1, 32, 15, 2048))
    return tuple(np.ascontiguousarray(o, dtype=np.float32) for o in outs)
```

```python
import numpy as np
import concourse.bass as bass
import concourse.mybir as mybir
from concourse.bass_utils import run_bass_kernel_spmd

F32 = mybir.dt.float32
BF16 = mybir.dt.bfloat16
AF = mybir.ActivationFunctionType
ALU = mybir.AluOpType
AX = mybir.AxisListType

D = 4096
KC = 32
TP = 1024
NS = 4
T = TP + NS
TH = 1024
DFF = 11008
FC = DFF // 128
HALO = (128, 512, 1024)
DIL = (1, 4, 16)
WIN = (128, 512, 2048)
SCALE = 128.0 ** -0.5
BIG = 1.0e6
SLOPES = [[2.0 ** (-8.0 * (g * 8 + h + 1) / 24.0) for h in range(8)] for g in range(3)]
OWN_PIECES = ((0, 343), (343, 343), (686, 342))
TBLK = [(i * 128, 128) for i in range(8)] + [(1024, 4)]
THIRDS = ((0, 29), (29, 29), (58, 28))


class Tok:
    __slots__ = ("w", "r", "dsem", "dcnt", "multi", "excl")

    def __init__(self, multi=False, excl=False):
        self.excl = excl
        self.w = {}
        self.r = {}
        self.dsem = None
        self.dcnt = 0
        self.multi = multi


class Eng:
    def __init__(self, nc, eng, name, same_wait):
        self.eng = eng
        self.sem = nc.alloc_semaphore("sem_" + name)
        self.cnt = 0
        self.seen = {}
        self.same_wait = same_wait

    def pending(self, dep, raw=True):
        sem, c = dep
        if sem is self.sem and not (self.same_wait and raw):
            return False
        return self.seen.get(id(sem), 0) < c

    def need(self, dep):
        if self.pending(dep):
            sem, c = dep
            self.eng.wait_ge(sem, c)
            self.seen[id(sem)] = c

    def wait_all(self, deps_raw, deps_other, ins_fn):
        todo = {}
        for d in deps_raw:
            if self.pending(d, True):
                k = id(d[0])
                if k not in todo or todo[k][1] < d[1]:
                    todo[k] = d
        for d in deps_other:
            if self.pending(d, True):
                k = id(d[0])
                if k not in todo or todo[k][1] < d[1]:
                    todo[k] = d
        lst = list(todo.values())
        for (sem, c) in lst[:-1]:
            self.eng.wait_ge(sem, c)
            self.seen[id(sem)] = c
        ins = ins_fn()
        if lst:
            sem, c = lst[-1]
            ins._wait_ge(sem, c)
            self.seen[id(sem)] = c
        return ins


class FW:
    def __init__(self, nc):
        self.nc = nc
        self.PE = Eng(nc, nc.tensor, "pe", False)
        self.ACT = Eng(nc, nc.scalar, "act", True)
        self.DVE = Eng(nc, nc.vector, "dve", True)
        self.POOL = Eng(nc, nc.gpsimd, "pool", True)
        self.SP = Eng(nc, nc.sync, "sp", True)
        self.nsem = 0
        self.dtoks = []

    @staticmethod
    def _deps(rt, wt):
        raw, other = [], []
        for t in rt:
            raw.extend(t.w.values())
            if t.excl:
                other.extend(t.r.values())
        for t in wt:
            other.extend(t.w.values())
            other.extend(t.r.values())
        return raw, other

    def op(self, e, fn, rt=(), wt=(), signal=True):
        raw, other = self._deps(rt, wt)
        ins = e.wait_all(raw, other, fn)
        if signal:
            e.cnt += 1
            ins.then_inc(e.sem, 1)
            stamp = (e.sem, e.cnt)
        else:
            stamp = (e.sem, e.cnt + 1)
        for t in wt:
            t.w = {id(e.sem): stamp}
            t.r = {}
        for t in rt:
            t.r[id(e.sem)] = stamp
        return ins

    def dma(self, q, out, in_, rt=(), wt=(), st=None):
        if st is None:
            st = wt[0] if wt else rt[0]
        if st.dsem is None:
            st.dsem = self.nc.alloc_semaphore("dsem%d" % self.nsem)
            self.nsem += 1
            self.dtoks.append(st)
        raw, other = self._deps(rt, wt)
        st.dcnt += 16
        ins = q.wait_all(raw + other, [], lambda: q.eng.dma_start(out=out, in_=in_))
        ins.then_inc(st.dsem, 16)
        stamp = (st.dsem, st.dcnt)
        for t in wt:
            if t.multi:
                t.w[id(st.dsem)] = stamp
            else:
                t.w = {id(st.dsem): stamp}
                t.r = {}
        for t in rt:
            t.r[id(st.dsem)] = stamp


CARRY = {}


def finalize(fw):
    for t in fw.dtoks:
        fw.SP.need((t.dsem, t.dcnt))
    for e in (fw.PE, fw.ACT, fw.DVE, fw.POOL):
        if e.cnt:
            fw.SP.need((e.sem, e.cnt))


class Stack:
    def __init__(self, nc, side):
        self.nc = nc
        self.side = side
        self.items = []
        self.n = 0

    def push(self, shape, dt):
        self.n += 1
        g = self.nc.sbuf_tensor("%s%d" % (self.side[0], self.n), list(shape), dt, side=self.side)
        t = g.__enter__()
        tk = Tok()
        tk.r = dict(CARRY)
        self.items.append((g, tk))
        return t, tk

    def mark(self):
        return len(self.items)

    def pop_to(self, mark):
        while len(self.items) > mark:
            g, tk = self.items.pop()
            for dd in (tk.w, tk.r):
                for k, (sem, c) in dd.items():
                    if k not in CARRY or CARRY[k][1] < c:
                        CARRY[k] = (sem, c)
            g.__exit__(None, None, None)


def build(debug=False, stop=99):
    CARRY.clear()
    nc = bass.Bass("TRN2", target_bir_lowering=False)
    fw = FW(nc)
    PE, ACT, DVE, POOL, SP = fw.PE, fw.ACT, fw.DVE, fw.POOL, fw.SP
    op, dma = fw.op, fw.dma
    L = Stack(nc, "left")
    R = Stack(nc, "right")

    def din(name, shape):
        return nc.dram_tensor(name, list(shape), F32, kind="ExternalInput").ap()

    def dout(name, shape):
        return nc.dram_tensor(name, list(shape), F32, kind="ExternalOutput").ap()

    def dscr(name, shape, dt):
        kind = "ExternalOutput" if debug else "Internal"
        return nc.dram_tensor(name, list(shape), dt, kind=kind).ap()

    x_own = din("x_own", [TP, D]); x_smp = din("x_smp", [NS, D]); x_halo = din("x_halo", [TH, D])
    w_in = din("w_in", [D, 19456]); w_up = din("w_up", [1024, D]); w_pool = din("w_pool", [4, 512, 1024])
    if stop >= 5:
        w_out = din("w_out", [D, D]); w_gate = din("w_gate", [D, DFF]); w_upf = din("w_upf", [D, DFF])
        w_down = din("w_down", [DFF, D])
    g_mix = din("g_mix", [128, KC]); g_ffn = din("g_ffn", [128, KC]); pscale_d = din("pscale", [128, KC])
    g_fin = din("g_fin", [1, D])
    ck = [din("ck%d" % g, [NS, WIN[g] * 1024]) for g in range(3)]
    cv = [din("cv%d" % g, [NS, WIN[g] * 1024]) for g in range(3)]
    spool = din("spool", [NS * 15, 2048])
    ident_d = din("ident", [128, 128])
    Da_d = din("Da", [128, 256]); Db_d = din("Db", [128, 256]); Dc_d = din("Dc", [64, 128]); Ds_d = din("Ds", [1, 129])
    invc_d = din("invc", [128, 4, 16])

    y_own = dout("y_own", [TP, D]); y_smp = dout("y_smp", [NS, D])
    NBLK = (1, 4, 8)
    kco = [dout("kc%d" % g, [NBLK[g] * 128, 1024]) for g in range(3)]
    vco = [dout("vc%d" % g, [NBLK[g] * 128, 1024]) for g in range(3)]
    pool_p = dout("pool_p", [15, 2048])
    kso = [dout("ks%d" % g, [NS, WIN[g] * 1024]) for g in range(3)]
    vso = [dout("vs%d" % g, [NS, WIN[g] * 1024]) for g in range(3)]
    pool_s = dout("pool_s", [NS, 15 * 2048])

    kvT_d = dscr("kvT_d", [48, 128, 2052], BF16); t_kvd = [Tok() for _ in range(48)]
    mixT_d = dscr("mixT_d", [32, 128, T], BF16); t_mixd = Tok(True)
    xa_d = dscr("xa_d", [T, D], F32); t_xa = Tok(True)
    xb_d = dscr("xb_d", [T, D], F32); t_xb = Tok(True)
    t_out = Tok(True)
    t_cc = Tok(True)

    def ps(name, shape, dt):
        return nc.alloc_psum_tensor(name, list(shape), dt), Tok(excl=True)

    main = [ps("mb%d" % i, [128, 512], F32) for i in range(4)]
    _X = [nc.alloc_psum_tensor("attX%d" % i, [128, 512], F32) for i in range(2)]
    _Y = [nc.alloc_psum_tensor("attY%d" % i, [128, 512], F32) for i in range(2)]
    tX = [Tok(excl=True), Tok(excl=True)]
    tY = [Tok(excl=True), Tok(excl=True)]
    ps_s = [(_X[u][:, 0:256], tX[u]) for u in range(2)]
    ps_t = [(_X[u][:, 256:384].bitcast(BF16), tX[u]) for u in range(2)]
    ps_v = [(_X[u][:, 384:512].bitcast(BF16), tX[u]) for u in range(2)]
    ps_o = [(_Y[u][:, 0:128], tY[u]) for u in range(2)]
    ps_l = [(_Y[u][:, 128:256], tY[u]) for u in range(2)]
    ps_r = [(_Y[u][:, 256:384], tY[u]) for u in range(2)]
    ps_p = [(_Y[u][:, 384:512], tY[u]) for u in range(2)]
    ps_x = [(_X[0][:, 0:256].bitcast(BF16).rearrange("p (a b) -> p a b", b=128), tX[0])]
    mstate = [0]

    def next_bank():
        import os
        if os.environ.get('NOREUSE') and mstate[0] >= 4:
            return (_X[1], tX[1])
        b = main[mstate[0] % 4]
        mstate[0] += 1
        return b

    ident_f, t_idf = L.push([128, 128], F32)
    ident_b, t_idb = L.push([128, 128], BF16)
    ones_b, t_1b = L.push([128, 128], BF16)
    ones_f, t_1f = L.push([128, 128], F32)
    eps_t, t_eps = L.push([128, 1], F32)
    Da, t_Da = L.push([128, 256], F32); Db, t_Db = L.push([128, 256], F32)
    Dc, t_Dc = L.push([64, 128], F32); Ds, t_Ds = L.push([1, 129], F32)
    gmix, t_gm = L.push([128, KC], F32); gffn, t_gf = L.push([128, KC], F32); pscale, t_psc = L.push([128, KC], F32)
    invc, t_invc = L.push([128, 4, 16], F32)
    uhalo, t_uh = L.push([128, 16, 16], F32)
    knew = [[L.push([128, 8, NS], F32) for kv in range(2)] for g in range(3)]
    ss1, t_ss1 = L.push([128, 9, 8], F32)
    ss3, t_ss3 = L.push([128, 9, 8], F32)
    dma(SP, ident_f[:], ident_d, wt=[t_idf])
    dma(POOL, ident_b[:], ident_d, wt=[t_idb])
    op(DVE, lambda: nc.vector.memset(ones_b[:], 1.0), wt=[t_1b])
    op(DVE, lambda: nc.vector.memset(ones_f[:], 1.0), wt=[t_1f])
    op(DVE, lambda: nc.vector.memset(eps_t[:], 1e-6), wt=[t_eps])
    for (sb_, tk_, dr_) in ((Da, t_Da, Da_d), (Db, t_Db, Db_d), (Dc, t_Dc, Dc_d), (Ds, t_Ds, Ds_d),
                            (gmix, t_gm, g_mix), (gffn, t_gf, g_ffn), (pscale, t_psc, pscale_d), (invc, t_invc, invc_d)):
        dma(SP, sb_[:], dr_, wt=[tk_])

    for g in range(3):
        n_el = (WIN[g] - 1) * 1024
        for (src, dst) in ((ck[g], kso[g]), (cv[g], vso[g])):
            for b in range(NS):
                dma(ACT, dst[b, 0:n_el].rearrange("(a f) -> a f", a=16),
                    src[b, 1024:1024 + n_el].rearrange("(a f) -> a f", a=16), wt=[t_cc])
    for b in range(NS):
        dma(ACT, pool_s[b, 0:14 * 2048].rearrange("(a f) -> a f", a=16),
            spool[b * 15 + 1:b * 15 + 15, :].rearrange("r c -> (r c)").rearrange("(a f) -> a f", a=16), wt=[t_cc])

    class Slots:
        def __init__(self, n, shape):
            self.tiles = [R.push(shape, BF16) for _ in range(n)]
            self.i = 0
            self.pending = [False] * n

        def load(self, src_ap):
            k = self.i % len(self.tiles)
            self.i += 1
            assert not self.pending[k], "slot reloaded before use"
            self.pending[k] = True
            t, tk = self.tiles[k]
            dma(POOL, t[:], src_ap, wt=[tk])
            return k

        def use(self, k):
            self.pending[k] = False
            return self.tiles[k]

    def wcols(w, c0, n, r0=0, nkc=KC):
        return w[r0:r0 + nkc * 128, c0:c0 + n].rearrange("(kc p) n -> p kc n", p=128)

    class Job:
        def __init__(self, load, compute):
            self.load = load
            self.compute = compute

    def run_jobs(jobs, pf=3):
        n = len(jobs)
        loaded = 0
        for i in range(n):
            while loaded < min(n, i + pf + 1):
                jobs[loaded].slot = jobs[loaded].load() if jobs[loaded].load else None
                loaded += 1
            jobs[i].compute(jobs[i].slot)

    def gemm_gen(wt_tile, wt_tok, nkc, movs, ep, act_toks):
        for pi, (rhs_fn, n) in enumerate(movs):
            bank, t_bank = next_bank()
            for kc in range(nkc):
                op(PE, lambda kc=kc: nc.tensor.matmul(bank[:, 0:n], lhsT=wt_tile[:, kc, :], rhs=rhs_fn(kc),
                                                      start=(kc == 0), stop=(kc == nkc - 1)),
                   rt=[wt_tok] + act_toks, wt=[t_bank], signal=(kc == nkc - 1))
                if kc == nkc - 1:
                    ep(pi, bank, t_bank, n)
                yield

    def gemm(wt_tile, wt_tok, nkc, movs, ep, act_toks):
        for pi, (rhs_fn, n) in enumerate(movs):
            bank, t_bank = next_bank()
            for kc in range(nkc):
                op(PE, lambda kc=kc: nc.tensor.matmul(bank[:, 0:n], lhsT=wt_tile[:, kc, :], rhs=rhs_fn(kc),
                                                      start=(kc == 0), stop=(kc == nkc - 1)),
                   rt=[wt_tok] + act_toks, wt=[t_bank], signal=(kc == nkc - 1))
            ep(pi, bank, t_bank, n)

    def norm_T(blocks, gtab, t_g, dstT, t_dst, xin, t_xin, xs, t_xs, ssq, t_ssq, rstd, t_rstd, pre_ss=None):
        for (src, n, col0, src_tok) in blocks:
            dma(SP, xin[0:n, :], src, rt=[src_tok] if src_tok else [], wt=[t_xin])
            if pre_ss is None:
                op(ACT, lambda: nc.scalar.activation(out=xs[0:n, :], in_=xin[0:n, :], func=AF.Square,
                                                     accum_out=ssq[0:n, 0:1]), rt=[t_xin], wt=[t_xs, t_ssq])
            else:
                tb_i = pre_ss[1]
                op(DVE, lambda: nc.vector.reduce_sum(out=ssq[0:n, 0:1], in_=pre_ss[0][0:n, tb_i(col0), :], axis=AX.X),
                   rt=[pre_ss[2]], wt=[t_ssq])
            op(ACT, lambda: nc.scalar.activation(out=rstd[0:n, 0:1], in_=ssq[0:n, 0:1], func=AF.Sqrt,
                                                 scale=1.0 / D, bias=eps_t[0:n, 0:1]), rt=[t_ssq, t_eps], wt=[t_rstd])
            op(DVE, lambda: nc.vector.reciprocal(out=rstd[0:n, 0:1], in_=rstd[0:n, 0:1]), rt=[t_rstd], wt=[t_rstd])
            op(DVE, lambda: nc.vector.tensor_scalar(out=xs[0:n, :], in0=xin[0:n, :], scalar1=rstd[0:n, 0:1],
                                                    scalar2=None, op0=ALU.mult), rt=[t_xin, t_rstd], wt=[t_xs])
            px, t_px = ps_x[0]
            for q in range(8):
                for i in range(4):
                    kc = 4 * q + i
                    op(PE, lambda kc=kc, i=i: nc.tensor.transpose(out=px[:, i, 0:n], in_=xs[0:n, kc * 128:(kc + 1) * 128],
                                                                  identity=ident_b[0:n, 0:n]),
                       rt=[t_xs, t_idb], wt=[t_px], signal=(i == 3))
                op(DVE, lambda q=q: nc.vector.tensor_tensor(
                    out=dstT[:, 4 * q:4 * q + 4, col0:col0 + n], in0=px[:, :, 0:n],
                    in1=gtab[:, 4 * q:4 * q + 4].unsqueeze(2).to_broadcast([128, 4, n]), op=ALU.mult),
                   rt=[t_px, t_g], wt=[t_dst])

    hT, t_hT = L.push([128, KC, T], BF16)
    wslots = Slots(4, [128, KC, 128])
    mRh = R.mark()
    hTh, t_hTh = R.push([128, KC, TH], BF16)
    mR = R.mark()
    xin, t_xin = R.push([128, D], F32)
    xs, t_xs = R.push([128, D], BF16)
    ssq, t_ssq = R.push([128, 1], F32)
    rstd, t_rstd = R.push([128, 1], F32)
    blocks_h = [(x_halo[i * 128:(i + 1) * 128, :], 128, i * 128, None) for i in range(8)]
    blocks_o = [(x_own[i * 128:(i + 1) * 128, :], 128, i * 128, None) for i in range(8)] + [(x_smp, NS, TP, None)]
    norm_T(blocks_h, gmix, t_gm, hTh, t_hTh, xin, t_xin, xs, t_xs, ssq, t_ssq, rstd, t_rstd)
    norm_T(blocks_o, gmix, t_gm, hT, t_hT, xin, t_xin, xs, t_xs, ssq, t_ssq, rstd, t_rstd)
    R.pop_to(mR)
    if stop == 1:
        finalize(fw)
        return nc

    def qkv_col(qkv, g, h):
        return ((qkv * 3 + g) * 8 + h) * 128

    kvst = [R.push([128, 2052], BF16) for _ in range(2)]
    st32 = [R.push([128, T], F32) for _ in range(2)]
    tokst = [R.push([128, 8, 128], F32) for _ in range(2)]
    p2n = [0]

    def kv_job(kv, g, h):
        ci = (kv - 1) * 24 + g * 8 + h
        hl = HALO[g]

        def load():
            return wslots.load(wcols(w_in, qkv_col(kv, g, h), 128))

        def compute(k):
            wt_tile, wt_tok = wslots.use(k)
            i2 = p2n[0] % 2
            p2n[0] += 1
            kst, t_kst = kvst[i2]
            s32, t_s32 = st32[i2]
            tks, t_tks = tokst[i2]
            movs = []
            dests = []
            for p0 in range(0, hl, 512):
                n = min(512, hl - p0)
                movs.append((lambda kc, a=TH - hl + p0, n=n: hTh[:, kc, a:a + n], n))
                dests.append((p0, None))
            for (c0, n) in OWN_PIECES:
                movs.append((lambda kc, c0=c0, n=n: hT[:, kc, c0:c0 + n], n))
                dests.append((hl + c0, c0))

            def ep(pi, bank, t_bank, n):
                d0, c0 = dests[pi]
                op(ACT, lambda: nc.scalar.copy(out=kst[:, d0:d0 + n], in_=bank[:, 0:n]), rt=[t_bank], wt=[t_kst])
                if c0 is not None:
                    op(DVE, lambda: nc.vector.tensor_copy(out=s32[:, c0:c0 + n], in_=bank[:, 0:n]), rt=[t_bank], wt=[t_s32])

            gemm(wt_tile, wt_tok, KC, movs, ep, [t_hT, t_hTh])
            import os
            if not os.environ.get('NOKVD'):
                dma(SP, kvT_d[ci, :, 0:hl + T], kst[:, 0:hl + T], rt=[t_kst], wt=[t_kvd[ci]], st=t_kst)
            blks = list(range(8 - NBLK[g], 8)) if not os.environ.get('NOTR') else []
            for bi, blk in enumerate(blks):
                pt, t_pt = ps_o[bi % 2]
                op(PE, lambda blk=blk: nc.tensor.transpose(out=pt[:, :], in_=s32[:, blk * 128:(blk + 1) * 128],
                                                           identity=ident_f[:, :]), rt=[t_s32, t_idf], wt=[t_pt])
                if bi % 2 == 0:
                    op(ACT, lambda bi=bi: nc.scalar.copy(out=tks[:, bi, :], in_=pt[:, :]), rt=[t_pt], wt=[t_tks])
                else:
                    op(DVE, lambda bi=bi: nc.vector.tensor_copy(out=tks[:, bi, :], in_=pt[:, :]), rt=[t_pt], wt=[t_tks])
            dst = (kco if kv == 1 else vco)[g]
            if blks:
              dma(SP, dst[:, h * 128:(h + 1) * 128].rearrange("(b p) d -> p b d", p=128), tks[:, 0:len(blks), :],
                rt=[t_tks], wt=[t_out], st=t_tks)
            kn, t_kn = knew[g][kv - 1]
            if not os.environ.get('NOKN'):
                op(DVE, lambda: nc.vector.tensor_copy(out=kn[:, h, :], in_=s32[:, TP:T]), rt=[t_s32], wt=[t_kn])

        return Job(load, compute)

    def uhalo_job(i):
        def load():
            return wslots.load(wcols(w_in, 9216 + i * 128, 128))

        def compute(k):
            wt_tile, wt_tok = wslots.use(k)

            def ep(pi, bank, t_bank, n):
                op(DVE, lambda: nc.vector.tensor_copy(out=uhalo[:, i, :], in_=bank[:, 0:16]), rt=[t_bank], wt=[t_uh])

            gemm(wt_tile, wt_tok, KC, [(lambda kc: hTh[:, kc, TH - 16:TH], 16)], ep, [t_hTh])

        return Job(load, compute)

    jobs2 = [kv_job(kv, g, h) for kv in (1, 2) for g in range(3) for h in range(8)] + [uhalo_job(i) for i in range(16)]

    P3 = {}

    def p3_begin(_k=None):
        R.pop_to(mRh)
        P3["attnT"] = L.push([128, 8, T], BF16)
        P3["QT"] = [R.push([128, T], BF16) for _ in range(2)]
        P3["KT"] = R.push([128, 2052], BF16)
        P3["VT"] = R.push([128, 2052], BF16)
        P3["Ksm"] = [[R.push([128, NS, 128], BF16) for g in range(3)] for _ in range(2)]
        P3["Vsm"] = [[R.push([128, NS, 128], BF16) for g in range(3)] for _ in range(2)]
        P3["KTs"] = [R.push([128, 132], BF16) for _ in range(4)]
        P3["S"] = [R.push([128, 256], F32) for _ in range(4)]
        P3["P"] = [R.push([128, 256], BF16) for _ in range(4)]
        P3["PT"] = [R.push([128, 2, 128], BF16) for _ in range(4)]
        P3["Vb"] = [R.push([128, 2, 128], BF16) for _ in range(4)]
        P3["st"] = [[R.push([128, 1], F32) for _ in range(6)] for _ in range(4)]
        P3["bcl"] = [R.push([128, 128], F32) for _ in range(4)]
        P3["bcr"] = [R.push([128, 128], F32) for _ in range(4)]
        P3["rd"] = [R.push([128, 128], F32) for _ in range(4)]
        P3["tmp"] = [R.push([128, 2], F32) for _ in range(4)]
        P3["og"] = [R.push([128, T], F32) for _ in range(3)]
        P3["lg"] = [R.push([128, T], F32) for _ in range(3)]
        P3["Lm"] = R.push([128, T], F32)
        P3["Wm"] = R.push([128, T], F32)
        P3["acc"] = R.push([128, T], F32)
        P3["un"] = 0

    def attn_unit(g, h, QTt, t_QT, q_sl, nq, KTsrc, t_KTsrc, k_sl, nk, vsrcs, Dtab, t_D, d_rows, out_sl, sample=None):
        return dict(g=g, h=h, QTt=QTt, t_QT=t_QT, q_sl=q_sl, nq=nq, KTsrc=KTsrc, t_KTsrc=t_KTsrc, k_sl=k_sl, nk=nk,
                    vsrcs=vsrcs, Dtab=Dtab, t_D=t_D, d_rows=d_rows, out_sl=out_sl, sample=sample)

    def _ctx(U):
        i = U["idx"]
        u2, u4 = i % 2, i % 4
        C = dict(U)
        C["c"] = -SLOPES[U["g"]][U["h"]] * DIL[U["g"]]
        C["pss"], C["t_X"] = ps_s[u2]
        C["pst"] = ps_t[u2][0]
        C["psv"] = ps_v[u2][0]
        C["pso"], C["t_Y"] = ps_o[u2]
        C["psl"] = ps_l[u2][0]
        C["psr"] = ps_r[u2][0]
        C["psp"] = ps_p[u2][0]
        for nm in ("S", "P", "PT", "Vb", "bcl", "bcr", "rd", "tmp"):
            C[nm], C["t_" + nm] = P3[nm][u4]
        C["st"] = P3["st"][u4]
        C["og"], C["t_og"] = P3["og"][U["g"]]
        C["lg"], C["t_lg"] = P3["lg"][U["g"]]
        C["nkb"] = len(U["vsrcs"])
        C["vT"] = (U["vsrcs"][0][0] == "T")
        return C

    def st_A(C):
        nkb, nq, nk = C["nkb"], C["nq"], C["nk"]
        if C["vT"]:
            for kb in range(nkb):
                op(PE, lambda kb=kb: nc.tensor.transpose(out=C["psv"][:, kb * 128:(kb + 1) * 128], in_=C["vsrcs"][kb][1],
                                                         identity=ident_b[:, :]), rt=[C["vsrcs"][kb][2], t_idb], wt=[C["t_X"]],
                   signal=False)
        op(PE, lambda: nc.tensor.matmul(C["pss"][0:nq, 0:nk], lhsT=C["QTt"][:, C["q_sl"]], rhs=C["KTsrc"][:, C["k_sl"]],
                                        start=True, stop=True), rt=[C["t_QT"], C["t_KTsrc"]], wt=[C["t_X"]])

    def st_B(C):
        nkb, nq, nk = C["nkb"], C["nq"], C["nk"]
        (m, t_m), (negm, t_negm), (den, t_den) = C["st"][0:3]
        S, Pm = C["S"], C["P"]
        if C["vT"]:
            op(ACT, lambda: nc.scalar.copy(out=C["Vb"][:, 0:nkb, :],
                                           in_=C["psv"][:, 0:nkb * 128].rearrange("p (b q) -> p b q", q=128)),
               rt=[C["t_X"]], wt=[C["t_Vb"]])
        op(DVE, lambda: nc.vector.scalar_tensor_tensor(out=S[0:nq, 0:nk], in0=C["Dtab"][C["d_rows"], 0:nk], scalar=C["c"],
                                                       in1=C["pss"][0:nq, 0:nk], op0=ALU.mult, op1=ALU.add),
           rt=[C["t_X"], C["t_D"]], wt=[C["t_S"]])
        op(DVE, lambda: nc.vector.reduce_max(out=m[0:nq, :], in_=S[0:nq, 0:nk], axis=AX.X), rt=[C["t_S"]], wt=[t_m])
        op(DVE, lambda: nc.vector.tensor_scalar(out=negm[0:nq, :], in0=m[0:nq, :], scalar1=-1.0, scalar2=None,
                                                op0=ALU.mult), rt=[t_m], wt=[t_negm])
        op(ACT, lambda: nc.scalar.activation(out=Pm[0:nq, 0:nk], in_=S[0:nq, 0:nk], func=AF.Exp, bias=negm[0:nq, :],
                                             scale=1.0, accum_out=den[0:nq, :]), rt=[C["t_S"], t_negm], wt=[C["t_P"], t_den])

    def st_C(C):
        nkb, nq = C["nkb"], C["nq"]
        for kb in range(nkb):
            op(PE, lambda kb=kb: nc.tensor.transpose(out=C["pst"][:, kb * 128:kb * 128 + nq],
                                                     in_=C["P"][0:nq, kb * 128:(kb + 1) * 128],
                                                     identity=ident_b[0:nq, 0:nq]), rt=[C["t_P"], t_idb], wt=[C["t_X"]],
               signal=(kb == nkb - 1))

    def st_D(C):
        nkb, nq = C["nkb"], C["nq"]
        (m, t_m), (negm, t_negm), (den, t_den), (lnd, t_lnd), (lse, t_lse), (rden, t_rden) = C["st"]
        op(ACT, lambda: nc.scalar.copy(out=C["PT"][:, 0:nkb, 0:nq],
                                       in_=C["pst"][:, 0:nkb * 128].rearrange("p (b q) -> p b q", q=128)[:, :, 0:nq]),
           rt=[C["t_X"]], wt=[C["t_PT"]])
        op(ACT, lambda: nc.scalar.activation(out=lnd[0:nq, :], in_=den[0:nq, :], func=AF.Ln), rt=[t_den], wt=[t_lnd])
        op(DVE, lambda: nc.vector.tensor_tensor(out=lse[0:nq, :], in0=lnd[0:nq, :], in1=m[0:nq, :], op=ALU.add),
           rt=[t_lnd, t_m], wt=[t_lse])
        op(DVE, lambda: nc.vector.reciprocal(out=rden[0:nq, :], in_=den[0:nq, :]), rt=[t_den], wt=[t_rden])
        op(DVE, lambda: nc.vector.tensor_scalar(out=C["bcl"][0:nq, :], in0=ones_f[0:nq, :], scalar1=lse[0:nq, 0:1], scalar2=None,
                                                op0=ALU.mult), rt=[t_lse, t_1f], wt=[C["t_bcl"]])
        op(DVE, lambda: nc.vector.tensor_scalar(out=C["bcr"][0:nq, :], in0=ones_f[0:nq, :], scalar1=rden[0:nq, 0:1], scalar2=None,
                                                op0=ALU.mult), rt=[t_rden, t_1f], wt=[C["t_bcr"]])

    def st_E(C):
        nkb, nq = C["nkb"], C["nq"]
        if C["vT"]:
            vmats = [(C["Vb"][:, kb, :], C["t_Vb"]) for kb in range(nkb)]
        else:
            vmats = [(C["vsrcs"][kb][1], C["vsrcs"][kb][2]) for kb in range(nkb)]
        for kb in range(nkb):
            op(PE, lambda kb=kb: nc.tensor.matmul(C["pso"][:, 0:nq], lhsT=vmats[kb][0], rhs=C["PT"][:, kb, 0:nq],
                                                  start=(kb == 0), stop=(kb == nkb - 1)),
               rt=[vmats[kb][1], C["t_PT"]], wt=[C["t_Y"]], signal=False)
        if C["sample"] is not None:
            op(PE, lambda: nc.tensor.matmul(C["psp"][:, 0:1], lhsT=ones_b[0:1, :], rhs=C["P"][0:1, 128:129], start=True, stop=True),
               rt=[C["t_P"], t_1b], wt=[C["t_Y"]], signal=False)
        op(PE, lambda: nc.tensor.matmul(C["psl"][:, 0:nq], lhsT=C["bcl"][0:nq, :], rhs=ident_f[0:nq, 0:nq], start=True, stop=True),
           rt=[C["t_bcl"], t_idf], wt=[C["t_Y"]], signal=False)
        op(PE, lambda: nc.tensor.matmul(C["psr"][:, 0:nq], lhsT=C["bcr"][0:nq, :], rhs=ident_f[0:nq, 0:nq], start=True, stop=True),
           rt=[C["t_bcr"], t_idf], wt=[C["t_Y"]])

    def st_F(C):
        nq, out_sl = C["nq"], C["out_sl"]
        og, lg, rd = C["og"], C["lg"], C["rd"]
        op(ACT, lambda: nc.scalar.copy(out=rd[:, 0:nq], in_=C["psr"][:, 0:nq]), rt=[C["t_Y"]], wt=[C["t_rd"]])
        op(ACT, lambda: nc.scalar.copy(out=lg[:, out_sl], in_=C["psl"][:, 0:nq]), rt=[C["t_Y"]], wt=[C["t_lg"]])
        if C["sample"] is None:
            op(DVE, lambda: nc.vector.tensor_tensor(out=og[:, out_sl], in0=C["pso"][:, 0:nq], in1=rd[:, 0:nq], op=ALU.mult),
               rt=[C["t_Y"], C["t_rd"]], wt=[C["t_og"]])
        else:
            vnew_ap, t_vnew = C["sample"]
            tmp, t_tmp = C["tmp"], C["t_tmp"]
            op(DVE, lambda: nc.vector.tensor_tensor(out=tmp[:, 0:1], in0=C["psp"][:, 0:1], in1=vnew_ap, op=ALU.mult),
               rt=[C["t_Y"], t_vnew], wt=[t_tmp])
            op(DVE, lambda: nc.vector.tensor_tensor(out=tmp[:, 1:2], in0=C["pso"][:, 0:1], in1=tmp[:, 0:1], op=ALU.add),
               rt=[C["t_Y"], t_tmp], wt=[t_tmp])
            op(DVE, lambda: nc.vector.tensor_tensor(out=og[:, out_sl], in0=tmp[:, 1:2], in1=rd[:, 0:1], op=ALU.mult),
               rt=[t_tmp, C["t_rd"]], wt=[C["t_og"]])

    def run_units(units, filler=None, nfill=96):
        cs = []
        for U in units:
            U["idx"] = P3["un"]
            P3["un"] += 1
            cs.append(_ctx(U))
        n = len(cs)
        for it in range(n + 2):
            if 0 <= it - 2 < n:
                st_E(cs[it - 2])
            if 0 <= it - 1 < n:
                st_C(cs[it - 1])
            if it < n:
                st_A(cs[it])
            if filler is not None:
                for _ in range(nfill // (n + 2) + 1):
                    next(filler, None)
            if 0 <= it - 2 < n:
                st_F(cs[it - 2])
            if 0 <= it - 1 < n:
                st_D(cs[it - 1])
            if it < n:
                st_B(cs[it])

    QJ = {}

    def q_job(j, g, first=False):
        def load():
            return wslots.load(wcols(w_in, qkv_col(0, g, j), 128))

        def compute(k):
            if first:
                p3_begin()
            hl = HALO[g]
            KT, t_KT = P3["KT"]
            VT, t_VT = P3["VT"]
            QT, t_QT = P3["QT"][(j * 3 + g) % 2]
            dma(SP, KT[:, 0:hl + T], kvT_d[g * 8 + j, :, 0:hl + T], rt=[t_kvd[g * 8 + j]], wt=[t_KT])
            dma(SP, VT[:, 0:hl + T], kvT_d[24 + g * 8 + j, :, 0:hl + T], rt=[t_kvd[24 + g * 8 + j]], wt=[t_VT])
            Ksm, t_Ksm = P3["Ksm"][j % 2][g]
            Vsm, t_Vsm = P3["Vsm"][j % 2][g]
            dl = DIL[g]
            ckv = ck[g].rearrange("b (r c) -> b r c", c=1024)
            cvv = cv[g].rearrange("b (r c) -> b r c", c=1024)
            dma(POOL, Ksm[:, :, :], ckv[:, 0:WIN[g]:dl, j * 128:(j + 1) * 128].rearrange("b r d -> r b d"), wt=[t_Ksm])
            dma(POOL, Vsm[:, :, :], cvv[:, 0:WIN[g]:dl, j * 128:(j + 1) * 128].rearrange("b r d -> r b d"), wt=[t_Vsm])

            def q_gemm(jj, gg, slot_k):
                wtt, wtk = wslots.use(slot_k)
                QTn, t_QTn = P3["QT"][(jj * 3 + gg) % 2]

                def ep(pi, bank, t_bank, n):
                    c0 = OWN_PIECES[pi][0]
                    op(ACT, lambda: nc.scalar.activation(out=QTn[:, c0:c0 + n], in_=bank[:, 0:n], func=AF.Copy, scale=SCALE),
                       rt=[t_bank], wt=[t_QTn])

                return gemm_gen(wtt, wtk, KC, [(lambda kc, c0=c0, n=n: hT[:, kc, c0:c0 + n], n) for (c0, n) in OWN_PIECES],
                                ep, [t_hT])

            if not QJ.get((j, g)):
                for _ in q_gemm(j, g, k):
                    pass
            nidx = j * 3 + g + 1
            filler = None
            if nidx < 24:
                nj, ng = nidx // 3, nidx % 3
                filler = q_gemm(nj, ng, jobs3[nidx].slot)
                QJ[(nj, ng)] = True
            for b in range(NS):
                KTs, t_KTs = P3["KTs"][b]
                psv, t_psv = ps_v[b % 2]
                op(PE, lambda: nc.tensor.transpose(out=psv[:, 0:128], in_=Ksm[:, b, :], identity=ident_b[:, :]),
                   rt=[t_Ksm, t_idb], wt=[t_psv])
                op(DVE, lambda: nc.vector.tensor_copy(out=KTs[:, 0:128], in_=psv[:, 0:128]), rt=[t_psv], wt=[t_KTs])
                op(DVE, lambda: nc.vector.tensor_copy(out=KTs[:, 128:129], in_=KT[:, hl + TP + b:hl + TP + b + 1]),
                   rt=[t_KT], wt=[t_KTs])
            units = []
            if g == 0:
                for i in range(8):
                    units.append(attn_unit(g, j, QT, t_QT, slice(128 * i, 128 * i + 128), 128, KT, t_KT, slice(128 * i, 128 * i + 256), 256,
                                           [("T", VT[:, 128 * i + 128 * kb:128 * i + 128 * kb + 128], t_VT) for kb in range(2)],
                                           Db if i == 0 else Da, t_Db if i == 0 else t_Da, slice(0, 128), slice(128 * i, 128 * i + 128)))
            elif g == 1:
                for r in range(4):
                    for blk in range(2):
                        b0 = r + 512 * blk
                        units.append(attn_unit(g, j, QT, t_QT, slice(b0, b0 + 512, 4), 128, KT, t_KT, slice(b0, b0 + 1024, 4), 256,
                                               [("T", VT[:, b0 + 512 * kb:b0 + 512 * kb + 512:4], t_VT) for kb in range(2)],
                                               Db if blk == 0 else Da, t_Db if blk == 0 else t_Da, slice(0, 128), slice(b0, b0 + 512, 4)))
            else:
                for r in range(16):
                    units.append(attn_unit(g, j, QT, t_QT, slice(r, 1024, 16), 64, KT, t_KT, slice(r, 2048, 16), 128,
                                           [("T", VT[:, r:2048:16], t_VT)], Dc, t_Dc, slice(0, 64), slice(r, 1024, 16)))
            for b in range(NS):
                KTs, t_KTs = P3["KTs"][b]
                units.append(attn_unit(g, j, QT, t_QT, slice(TP + b, TP + b + 1), 1, KTs, t_KTs, slice(0, 129), 129,
                                       [("N", Vsm[:, b, :], t_Vsm)], Ds, t_Ds, slice(0, 1), slice(TP + b, TP + b + 1),
                                       sample=(VT[:, hl + TP + b:hl + TP + b + 1], t_VT)))
            run_units(units, filler)
            if filler is not None:
                for _ in filler:
                    pass
            if g == 2:
                merge(j)

        return Job(load, compute)

    def merge(j):
        (o0, t_o0), (o1, t_o1), (o2, t_o2) = P3["og"]
        (l0, t_l0), (l1, t_l1), (l2, t_l2) = P3["lg"]
        Lm, t_L = P3["Lm"]
        Wm, t_W = P3["Wm"]
        acc, t_acc = P3["acc"]
        attnT, t_attnT = P3["attnT"]
        V = nc.vector
        op(DVE, lambda: V.tensor_tensor(out=Lm[:], in0=l0[:], in1=l1[:], op=ALU.max), rt=[t_l0, t_l1], wt=[t_L])
        op(DVE, lambda: V.tensor_tensor(out=Lm[:], in0=Lm[:], in1=l2[:], op=ALU.max), rt=[t_L, t_l2], wt=[t_L])
        for (l, t_l) in ((l0, t_l0), (l1, t_l1), (l2, t_l2)):
            op(DVE, lambda l=l: V.tensor_tensor(out=l[:], in0=l[:], in1=Lm[:], op=ALU.subtract), rt=[t_l, t_L], wt=[t_l])
            op(ACT, lambda l=l: nc.scalar.activation(out=l[:], in_=l[:], func=AF.Exp), rt=[t_l], wt=[t_l])
        op(DVE, lambda: V.tensor_tensor(out=Wm[:], in0=l0[:], in1=l1[:], op=ALU.add), rt=[t_l0, t_l1], wt=[t_W])
        op(DVE, lambda: V.tensor_tensor(out=Wm[:], in0=Wm[:], in1=l2[:], op=ALU.add), rt=[t_W, t_l2], wt=[t_W])
        op(DVE, lambda: V.reciprocal(out=Wm[:], in_=Wm[:]), rt=[t_W], wt=[t_W])
        op(DVE, lambda: V.tensor_tensor(out=acc[:], in0=o0[:], in1=l0[:], op=ALU.mult), rt=[t_o0, t_l0], wt=[t_acc])
        op(DVE, lambda: V.tensor_tensor(out=o1[:], in0=o1[:], in1=l1[:], op=ALU.mult), rt=[t_o1, t_l1], wt=[t_o1])
        op(DVE, lambda: V.tensor_tensor(out=acc[:], in0=acc[:], in1=o1[:], op=ALU.add), rt=[t_acc, t_o1], wt=[t_acc])
        op(DVE, lambda: V.tensor_tensor(out=o2[:], in0=o2[:], in1=l2[:], op=ALU.mult), rt=[t_o2, t_l2], wt=[t_o2])
        op(DVE, lambda: V.tensor_tensor(out=acc[:], in0=acc[:], in1=o2[:], op=ALU.add), rt=[t_acc, t_o2], wt=[t_acc])
        op(DVE, lambda: V.tensor_tensor(out=attnT[:, j, :], in0=acc[:], in1=Wm[:], op=ALU.mult), rt=[t_acc, t_W], wt=[t_attnT])

    jobs3 = [q_job(j, g, first=(j == 0 and g == 0)) for j in range(8) for g in range(3)]

    P4 = {}

    def p4_begin():
        attn_keep = P3["attnT"]
        R.pop_to(mRh)
        P4["attnT"] = attn_keep
        P4["wup"] = Slots(2, [128, 8, 128])
        P4["wpl"] = Slots(2, [128, 4, 128])
        P4["U0"] = R.push([128, 16 + TP], F32)
        P4["T1"] = R.push([128, 16 + TP], F32)
        P4["T2"] = R.push([128, 16 + TP], F32)
        P4["pooled"] = R.push([128, 4, T], BF16)
        P4["sgA"] = R.push([128, T], F32)
        P4["sgB"] = R.push([128, T], F32)
        P4["t1"] = R.push([128, T], F32)
        P4["t2"] = R.push([128, T], F32)
        P4["mixst"] = [R.push([128, T], BF16) for _ in range(2)]
        P4["hist"] = R.push([128, 16, NS, 16], F32)
        P4["sprow"] = R.push([NS * 15, 2048], F32)
        P4["uprow"] = R.push([15, 2048], F32)
        P4["usrow"] = R.push([NS, 2048], F32)
        P4["ssum"] = R.push([128, NS], F32)
        sprow, t_sp = P4["sprow"]
        hist, t_hist = P4["hist"]
        dma(SP, sprow[:, :], spool, wt=[t_sp])
        for chn in range(16):
            pt, t_pt = ps_o[chn % 2]
            op(PE, lambda chn=chn: nc.tensor.transpose(out=pt[:, 0:NS * 15], in_=sprow[:, chn * 128:(chn + 1) * 128],
                                                       identity=ident_f[0:NS * 15, 0:NS * 15]), rt=[t_sp, t_idf], wt=[t_pt])
            op(ACT, lambda chn=chn: nc.scalar.copy(out=hist[:, chn, :, 0:15],
                                                   in_=pt[:, 0:NS * 15].rearrange("p (b r) -> p b r", r=15)),
               rt=[t_pt], wt=[t_hist])

    PWIN = (2, 4, 8, 16)

    def u_job(pg, i, first=False):
        chn = pg * 4 + i

        def load():
            return wslots.load(wcols(w_in, 9216 + chn * 128, 128))

        def compute(k):
            if first:
                p4_begin()
            wt_tile, wt_tok = wslots.use(k)
            U0, t_U0 = P4["U0"]
            T1, t_T1 = P4["T1"]
            T2, t_T2 = P4["T2"]
            pooled, t_pooled = P4["pooled"]
            hist, t_hist = P4["hist"]
            uprow, t_upr = P4["uprow"]
            usrow, t_usr = P4["usrow"]
            ssum, t_ssum = P4["ssum"]
            w = PWIN[pg]
            V = nc.vector
            op(DVE, lambda: V.tensor_copy(out=U0[:, 0:16], in_=uhalo[:, chn, :]), rt=[t_uh], wt=[t_U0])

            def ep(pi, bank, t_bank, n):
                c0 = OWN_PIECES[pi][0]
                npr = min(n, TP - c0)
                op(ACT, lambda: nc.scalar.copy(out=U0[:, 16 + c0:16 + c0 + npr], in_=bank[:, 0:npr]), rt=[t_bank], wt=[t_U0])
                if npr < n:
                    op(DVE, lambda: V.tensor_copy(out=hist[:, chn, :, 15], in_=bank[:, npr:n]), rt=[t_bank], wt=[t_hist])

            gemm(wt_tile, wt_tok, KC, [(lambda kc, c0=c0, n=n: hT[:, kc, c0:c0 + n], n) for (c0, n) in OWN_PIECES], ep, [t_hT])
            NU = 16 + TP
            src, t_src = U0, t_U0
            sh = 1
            bufs = [(T1, t_T1), (T2, t_T2)]
            bi = 0
            while sh < w:
                dst, t_dst = bufs[bi % 2]
                bi += 1
                lo = 2 * sh - 1
                op(DVE, lambda src=src, dst=dst, sh=sh, lo=lo: V.tensor_tensor(out=dst[:, lo:NU], in0=src[:, lo:NU],
                                                                               in1=src[:, lo - sh:NU - sh], op=ALU.add),
                   rt=[t_src], wt=[t_dst])
                src, t_src = dst, t_dst
                sh *= 2
            tot, t_tot = src, t_src
            op(DVE, lambda: V.scalar_tensor_tensor(out=pooled[:, i, 16:TP], in0=tot[:, 32:NU], scalar=1.0 / w,
                                                   in1=U0[:, 32:NU], op0=ALU.mult, op1=ALU.subtract),
               rt=[t_tot, t_U0], wt=[t_pooled])
            ot, t_ot = bufs[bi % 2]
            op(DVE, lambda: V.tensor_tensor(out=ot[:, 0:16], in0=tot[:, 16:32], in1=invc[:, pg, :], op=ALU.mult),
               rt=[t_tot, t_invc], wt=[t_ot])
            op(DVE, lambda: V.tensor_tensor(out=pooled[:, i, 0:16], in0=ot[:, 0:16], in1=U0[:, 16:32], op=ALU.subtract),
               rt=[t_ot, t_U0], wt=[t_pooled])
            op(DVE, lambda: V.tensor_reduce(out=ssum[:, :], in_=hist[:, chn, :, 16 - w:16], axis=AX.X, op=ALU.add),
               rt=[t_hist], wt=[t_ssum])
            op(DVE, lambda: V.scalar_tensor_tensor(out=pooled[:, i, TP:T], in0=ssum[:, :], scalar=1.0 / w,
                                                   in1=hist[:, chn, :, 15], op0=ALU.mult, op1=ALU.subtract),
               rt=[t_ssum, t_hist], wt=[t_pooled])
            pt, t_pt = ps_o[0]
            op(PE, lambda: nc.tensor.transpose(out=pt[0:15, :], in_=U0[:, 16 + TP - 15:16 + TP], identity=ident_f[:, :]),
               rt=[t_U0, t_idf], wt=[t_pt])
            op(ACT, lambda: nc.scalar.copy(out=uprow[:, chn * 128:(chn + 1) * 128], in_=pt[0:15, :]), rt=[t_pt], wt=[t_upr])
            pt2, t_pt2 = ps_o[1]
            op(PE, lambda: nc.tensor.transpose(out=pt2[0:NS, :], in_=hist[:, chn, :, 15], identity=ident_f[:, :]),
               rt=[t_hist, t_idf], wt=[t_pt2])
            op(ACT, lambda: nc.scalar.copy(out=usrow[:, chn * 128:(chn + 1) * 128], in_=pt2[0:NS, :]), rt=[t_pt2], wt=[t_usr])
            if chn == 15:
                dma(SP, pool_p, uprow[:, :], rt=[t_upr], wt=[t_out], st=t_upr)
                dma(SP, pool_s.rearrange("b (r c) -> b r c", c=2048)[:, 14, :], usrow[:, :], rt=[t_usr], wt=[t_out], st=t_usr)

        return Job(load, compute)

    def gate_job(c, which):
        def load():
            return wslots.load(wcols(w_in, 11264 + which * 4096 + c * 128, 128))

        def compute(k):
            wt_tile, wt_tok = wslots.use(k)
            sg, t_sg = P4["sgA"] if which == 0 else P4["sgB"]

            def ep(pi, bank, t_bank, n):
                c0 = OWN_PIECES[pi][0]
                op(ACT, lambda: nc.scalar.activation(out=sg[:, c0:c0 + n], in_=bank[:, 0:n], func=AF.Sigmoid),
                   rt=[t_bank], wt=[t_sg])

            gemm(wt_tile, wt_tok, KC, [(lambda kc, c0=c0, n=n: hT[:, kc, c0:c0 + n], n) for (c0, n) in OWN_PIECES], ep, [t_hT])

        return Job(load, compute)

    def a_job(c):
        def load():
            return P4["wup"].load(wcols(w_up, c * 128, 128, nkc=8))

        def compute(k):
            wt_tile, wt_tok = P4["wup"].use(k)
            attnT, t_attnT = P4["attnT"]
            sgA, t_sgA = P4["sgA"]
            t1, t_t1 = P4["t1"]

            def ep(pi, bank, t_bank, n):
                c0 = OWN_PIECES[pi][0]
                op(DVE, lambda: nc.vector.tensor_tensor(out=t1[:, c0:c0 + n], in0=bank[:, 0:n], in1=sgA[:, c0:c0 + n], op=ALU.mult),
                   rt=[t_bank, t_sgA], wt=[t_t1])

            gemm(wt_tile, wt_tok, 8, [(lambda kc, c0=c0, n=n: attnT[:, kc, c0:c0 + n], n) for (c0, n) in OWN_PIECES], ep, [t_attnT])

        return Job(load, compute)

    def pb_job(c):
        pg = c // 8

        def load():
            return P4["wpl"].load(w_pool[pg, :, (c % 8) * 128:(c % 8) * 128 + 128].rearrange("(kc p) n -> p kc n", p=128))

        def compute(k):
            wt_tile, wt_tok = P4["wpl"].use(k)
            pooled, t_pooled = P4["pooled"]
            sgB, t_sgB = P4["sgB"]
            t1, t_t1 = P4["t1"]
            t2, t_t2 = P4["t2"]
            mixst, t_mixst = P4["mixst"][c % 2]

            def ep(pi, bank, t_bank, n):
                c0 = OWN_PIECES[pi][0]
                op(DVE, lambda: nc.vector.scalar_tensor_tensor(out=t2[:, c0:c0 + n], in0=bank[:, 0:n], scalar=pscale[:, c:c + 1],
                                                               in1=sgB[:, c0:c0 + n], op0=ALU.mult, op1=ALU.mult),
                   rt=[t_bank, t_sgB, t_psc], wt=[t_t2])
                op(DVE, lambda: nc.vector.tensor_tensor(out=mixst[:, c0:c0 + n], in0=t1[:, c0:c0 + n], in1=t2[:, c0:c0 + n], op=ALU.add),
                   rt=[t_t1, t_t2], wt=[t_mixst])

            gemm(wt_tile, wt_tok, 4, [(lambda kc, c0=c0, n=n: pooled[:, kc, c0:c0 + n], n) for (c0, n) in OWN_PIECES], ep, [t_pooled])
            dma(SP, mixT_d[c], mixst[:, :], rt=[t_mixst], wt=[t_mixd], st=t_mixst)

        return Job(load, compute)

    jobs4 = []
    for pg in range(4):
        for i in range(4):
            jobs4.append(u_job(pg, i, first=(pg == 0 and i == 0)))
        for c in range(pg * 8, pg * 8 + 8):
            jobs4 += [gate_job(c, 0), a_job(c), gate_job(c, 1), pb_job(c)]

    if stop == 2:
        import os
        _nj = int(os.environ.get('P2NJ', '999'))
        run_jobs((jobs2[:_nj] + ([] if os.environ.get('NOUH') else jobs2[-1:])) if _nj < 999 else jobs2)
        finalize(fw)
        return nc
    if stop == 3:
        run_jobs(jobs2 + jobs3)
        finalize(fw)
        return nc
    run_jobs(jobs2 + jobs3 + jobs4)
    if stop == 4:
        finalize(fw)
        return nc

    rows_st, t_rows = R.push([32, 128], F32)
    for g in range(3):
        for kv in range(2):
            kn, t_kn = knew[g][kv]
            pt, t_pt = ps_o[(g * 2 + kv) % 2]
            op(PE, lambda: nc.tensor.transpose(out=pt[0:32, :], in_=kn[:, :, :].rearrange("p h b -> p (h b)"), identity=ident_f[:, :]),
               rt=[t_kn, t_idf], wt=[t_pt])
            op(ACT, lambda: nc.scalar.copy(out=rows_st[:, :], in_=pt[0:32, :]), rt=[t_pt], wt=[t_rows])
            dst = (kso if kv == 0 else vso)[g].rearrange("b (r h d) -> r h b d", h=8, d=128)[WIN[g] - 1]
            for h in range(8):
                dma(SP, dst[h], rows_st[4 * h:4 * h + 4, :], rt=[t_rows], wt=[t_out], st=t_rows)

    R.pop_to(0)
    L.pop_to(L.mark() - 2)
    mL = L.mark()
    mixT, t_mixT = L.push([128, KC, T], BF16)
    dma(SP, mixT[:, :, :], mixT_d.rearrange("c p t -> p c t"), rt=[t_mixd], wt=[t_mixT])
    woslots = Slots(2, [128, KC, 512])
    xsl = [R.push([128, 512], F32) for _ in range(2)]
    x1st = [R.push([128, 512], F32) for _ in range(2)]
    sqj, t_sqj = R.push([128, 512], BF16)
    p5n = [0]

    def rows_of(dr_own, dr_smp, c0, n):
        return dr_own[c0:c0 + n, :] if c0 < TP else dr_smp[0:n, :]

    def wo_job(nn):
        def load():
            return woslots.load(wcols(w_out, nn * 512, 512))

        def compute(k):
            wt_tile, wt_tok = woslots.use(k)
            for tb, (c0, n) in enumerate(TBLK):
                i2 = p5n[0] % 2
                p5n[0] += 1
                xs_, t_xs_ = xsl[i2]
                x1_, t_x1_ = x1st[i2]
                dma(SP, xs_[0:n, :], rows_of(x_own, x_smp, c0, n)[:, nn * 512:(nn + 1) * 512], wt=[t_xs_])
                bank, t_bank = next_bank()
                for kc in range(KC):
                    op(PE, lambda kc=kc: nc.tensor.matmul(bank[0:n, :], lhsT=mixT[:, kc, c0:c0 + n], rhs=wt_tile[:, kc, :],
                                                          start=(kc == 0), stop=(kc == KC - 1)),
                       rt=[wt_tok, t_mixT], wt=[t_bank], signal=(kc == KC - 1))
                op(DVE, lambda: nc.vector.tensor_tensor(out=x1_[0:n, :], in0=bank[0:n, :], in1=xs_[0:n, :], op=ALU.add),
                   rt=[t_bank, t_xs_], wt=[t_x1_])
                op(ACT, lambda: nc.scalar.activation(out=sqj[0:n, :], in_=x1_[0:n, :], func=AF.Square,
                                                     accum_out=ss1[0:n, tb, nn:nn + 1]), rt=[t_x1_], wt=[t_sqj, t_ss1])
                dma(SP, xa_d[c0:c0 + n, nn * 512:(nn + 1) * 512], x1_[0:n, :], rt=[t_x1_], wt=[t_xa], st=t_x1_)

        return Job(load, compute)

    run_jobs([wo_job(nn) for nn in range(8)], pf=1)
    if stop == 5:
        finalize(fw)
        return nc

    R.pop_to(0)
    L.pop_to(mL)
    hfT, t_hfT = L.push([128, KC, T], BF16)
    xin, t_xin = R.push([128, D], F32)
    xs, t_xs = R.push([128, D], BF16)
    ssq, t_ssq = R.push([128, 1], F32)
    rstd, t_rstd = R.push([128, 1], F32)
    blocks_x1 = [(xa_d[c0:c0 + n, :], n, c0, t_xa) for (c0, n) in TBLK]
    norm_T(blocks_x1, gffn, t_gf, hfT, t_hfT, xin, t_xin, xs, t_xs, ssq, t_ssq, rstd, t_rstd,
           pre_ss=(ss1, lambda col0: col0 // 128, t_ss1))
    R.pop_to(0)
    if stop == 6:
        finalize(fw)
        return nc

    ffT, t_ffT = L.push([128, 29, T], BF16)
    scr = [(xa_d, t_xa), (xb_d, t_xb)]
    for ti, (cc0, nch) in enumerate(THIRDS):
        mR6 = R.mark()
        gslots = Slots(4, [128, KC, 128])
        sgs = [R.push([128, T], F32) for _ in range(2)]

        def gu_job(c, which):
            def load():
                return gslots.load(wcols(w_gate if which == 0 else w_upf, c * 128, 128))

            def compute(k):
                wt_tile, wt_tok = gslots.use(k)
                sg, t_sg = sgs[c % 2]

                def ep(pi, bank, t_bank, n):
                    c0 = OWN_PIECES[pi][0]
                    if which == 0:
                        op(ACT, lambda: nc.scalar.activation(out=sg[:, c0:c0 + n], in_=bank[:, 0:n], func=AF.Silu),
                           rt=[t_bank], wt=[t_sg])
                    else:
                        op(DVE, lambda: nc.vector.tensor_tensor(out=ffT[:, c - cc0, c0:c0 + n], in0=bank[:, 0:n],
                                                                in1=sg[:, c0:c0 + n], op=ALU.mult),
                           rt=[t_bank, t_sg], wt=[t_ffT])

                gemm(wt_tile, wt_tok, KC, [(lambda kc, c0=c0, n=n: hfT[:, kc, c0:c0 + n], n) for (c0, n) in OWN_PIECES], ep, [t_hfT])

            return Job(load, compute)

        run_jobs([gu_job(c, which) for c in range(cc0, cc0 + nch) for which in (0, 1)])
        R.pop_to(mR6)
        wdslots = Slots(2, [128, 29, 512])
        psl_ = [R.push([128, 512], F32) for _ in range(2)]
        ost_ = [R.push([128, 512], F32) for _ in range(2)]
        sqj, t_sqj = R.push([128, 512], BF16)
        (src_d, t_src), (dst_d, t_dst) = scr[ti % 2], scr[(ti + 1) % 2]
        p6n = [0]
        last = (ti == 2)

        def wd_job(nn):
            def load():
                t, tk = wdslots.tiles[wdslots.i % 2]
                k = wdslots.i % 2
                wdslots.i += 1
                assert not wdslots.pending[k]
                wdslots.pending[k] = True
                dma(POOL, t[:, 0:nch, :], wcols(w_down, nn * 512, 512, r0=cc0 * 128, nkc=nch), wt=[tk])
                return k

            def compute(k):
                wt_tile, wt_tok = wdslots.use(k)
                for tb, (c0, n) in enumerate(TBLK):
                    i2 = p6n[0] % 2
                    p6n[0] += 1
                    pv_, t_pv_ = psl_[i2]
                    o_, t_o_ = ost_[i2]
                    dma(SP, pv_[0:n, :], src_d[c0:c0 + n, nn * 512:(nn + 1) * 512], rt=[t_src], wt=[t_pv_])
                    bank, t_bank = next_bank()
                    for kc in range(nch):
                        op(PE, lambda kc=kc: nc.tensor.matmul(bank[0:n, :], lhsT=ffT[:, kc, c0:c0 + n], rhs=wt_tile[:, kc, :],
                                                              start=(kc == 0), stop=(kc == nch - 1)),
                           rt=[wt_tok, t_ffT], wt=[t_bank], signal=(kc == nch - 1))
                    op(DVE, lambda: nc.vector.tensor_tensor(out=o_[0:n, :], in0=bank[0:n, :], in1=pv_[0:n, :], op=ALU.add),
                       rt=[t_bank, t_pv_], wt=[t_o_])
                    if last:
                        op(ACT, lambda: nc.scalar.activation(out=sqj[0:n, :], in_=o_[0:n, :], func=AF.Square,
                                                             accum_out=ss3[0:n, tb, nn:nn + 1]), rt=[t_o_], wt=[t_sqj, t_ss3])
                    dma(SP, dst_d[c0:c0 + n, nn * 512:(nn + 1) * 512], o_[0:n, :], rt=[t_o_], wt=[t_dst], st=t_o_)

            return Job(load, compute)

        run_jobs([wd_job(nn) for nn in range(8)], pf=1)
        R.pop_to(mR6)

    R.pop_to(0)
    L.pop_to(mL)
    fin_d, t_fin = scr[1]
    gfin, t_gfin = L.push([128, D], F32)
    dma(SP, gfin[:, :], g_fin.partition_broadcast(128), wt=[t_gfin])
    xin7 = [R.push([128, D], F32) for _ in range(2)]
    yst7 = [R.push([128, D], F32) for _ in range(2)]
    ssq, t_ssq = R.push([128, 1], F32)
    rstd, t_rstd = R.push([128, 1], F32)
    for tb, (c0, n) in enumerate(TBLK):
        xi, t_xi = xin7[tb % 2]
        yo, t_yo = yst7[tb % 2]
        dma(SP, xi[0:n, :], fin_d[c0:c0 + n, :], rt=[t_fin], wt=[t_xi])
        op(DVE, lambda: nc.vector.reduce_sum(out=ssq[0:n, 0:1], in_=ss3[0:n, tb, :], axis=AX.X), rt=[t_ss3], wt=[t_ssq])
        op(ACT, lambda: nc.scalar.activation(out=rstd[0:n, 0:1], in_=ssq[0:n, 0:1], func=AF.Sqrt, scale=1.0 / D,
                                             bias=eps_t[0:n, 0:1]), rt=[t_ssq, t_eps], wt=[t_rstd])
        op(DVE, lambda: nc.vector.reciprocal(out=rstd[0:n, 0:1], in_=rstd[0:n, 0:1]), rt=[t_rstd], wt=[t_rstd])
        op(DVE, lambda: nc.vector.scalar_tensor_tensor(out=yo[0:n, :], in0=xi[0:n, :], scalar=rstd[0:n, 0:1], in1=gfin[0:n, :],
                                                       op0=ALU.mult, op1=ALU.mult), rt=[t_xi, t_rstd, t_gfin], wt=[t_yo])
        dma(SP, rows_of(y_own, y_smp, c0, n), yo[0:n, :], rt=[t_yo], wt=[t_out], st=t_yo)
    finalize(fw)
    return nc


def _tables(first_half):
    qi = np.arange(128)[:, None]
    kj = np.arange(256)[None, :]
    delta = qi + 128 - kj
    valid = (delta >= 0) & (delta <= 128)
    Da = np.where(valid, delta, BIG).astype(np.float32)
    vb = valid & ~((kj < 128) & first_half)
    Db = np.where(vb, delta, BIG).astype(np.float32)
    qi = np.arange(64)[:, None]
    kj = np.arange(128)[None, :]
    delta = qi + 64 - kj
    vc = (delta >= 0) & ~((kj < 64) & first_half)
    Dc = np.where(vc, delta, BIG).astype(np.float32)
    Ds = np.concatenate([128 - np.arange(128), [0]]).astype(np.float32)[None, :]
    invc = np.zeros((128, 4, 16), np.float32)
    for g, w in enumerate((2, 4, 8, 16)):
        pos = np.arange(16)
        cnt = np.minimum(w, pos + 1) if first_half else np.full(16, w)
        invc[:, g, :] = (1.0 / cnt)[None, :]
    return Da, Db, Dc, Ds, invc


_NC_CACHE = {}


def _in_maps(x_prompt, x_sample, cache_k_w128, cache_v_w128, cache_k_w512, cache_v_w512, cache_k_w2048, cache_v_w2048,
             state_pool, norm_mix, w_in, w_up_attn, w_pool_map, pool_scale, w_out, norm_ffn, w_ffn_gate, w_ffn_up,
             w_ffn_down, norm_final, clist):
    f = lambda a: np.ascontiguousarray(np.asarray(a, dtype=np.float32))
    x_prompt = f(x_prompt); x_sample = f(x_sample)
    cks = [f(cache_k_w128)[0], f(cache_k_w512)[0], f(cache_k_w2048)[0]]
    cvs = [f(cache_v_w128)[0], f(cache_v_w512)[0], f(cache_v_w2048)[0]]
    sp = f(state_pool)[0]
    lay = lambda v: np.ascontiguousarray(f(v).reshape(KC, 128).T)
    shared = {
        "w_in": f(w_in)[0], "w_up": f(w_up_attn)[0], "w_pool": f(w_pool_map)[0], "w_out": f(w_out)[0],
        "w_gate": f(w_ffn_gate)[0], "w_upf": f(w_ffn_up)[0], "w_down": f(w_ffn_down)[0],
        "g_mix": lay(norm_mix), "g_ffn": lay(norm_ffn), "pscale": lay(pool_scale),
        "g_fin": f(norm_final).reshape(1, D), "ident": np.eye(128, dtype=np.float32),
    }
    in_maps = []
    for c in clist:
        b, half = c // 2, c % 2
        t0 = half * TP
        m = dict(shared)
        m["x_own"] = np.ascontiguousarray(x_prompt[b, t0:t0 + TP])
        m["x_smp"] = np.ascontiguousarray(x_sample[NS * c:NS * c + NS, 0])
        m["x_halo"] = np.ascontiguousarray(x_prompt[b, 0:TH]) if half else np.zeros((TH, D), np.float32)
        for g in range(3):
            m["ck%d" % g] = np.ascontiguousarray(cks[g][NS * c:NS * c + NS].reshape(NS, -1))
            m["cv%d" % g] = np.ascontiguousarray(cvs[g][NS * c:NS * c + NS].reshape(NS, -1))
        m["spool"] = np.ascontiguousarray(sp[NS * c:NS * c + NS].reshape(NS * 15, 2048))
        Da, Db, Dc, Ds, invc = _tables(half == 0)
        m.update(Da=Da, Db=Db, Dc=Dc, Ds=Ds, invc=invc)
        in_maps.append(m)
    return in_maps


def kernel(x_prompt, x_sample, cache_k_w128, cache_v_w128, cache_k_w512, cache_v_w512, cache_k_w2048, cache_v_w2048,
           state_pool, norm_mix, w_in, w_up_attn, w_pool_map, pool_scale, w_out, norm_ffn, w_ffn_gate, w_ffn_up,
           w_ffn_down, norm_final, _debug=False, _cores=8, _clist=None, _stop=99):
    in_maps = _in_maps(x_prompt, x_sample, cache_k_w128, cache_v_w128, cache_k_w512, cache_v_w512, cache_k_w2048,
                       cache_v_w2048, state_pool, norm_mix, w_in, w_up_attn, w_pool_map, pool_scale, w_out, norm_ffn,
                       w_ffn_gate, w_ffn_up, w_ffn_down, norm_final, _clist if _clist is not None else list(range(_cores)))
    if _stop < 5:
        for m in in_maps:
            for k in ("w_out", "w_gate", "w_upf", "w_down"):
                m.pop(k)
    key = (bool(_debug), _stop)
    if key not in _NC_CACHE:
        _NC_CACHE[key] = build(debug=_debug, stop=_stop)
    nc = _NC_CACHE[key]
    res = run_bass_kernel_spmd(nc, in_maps, core_ids=list(range(len(in_maps))))
    r = res.results
    if _debug:
        return r
    B = 4
    y_prompt = np.stack([np.concatenate([r[2 * b]["y_own"], r[2 * b + 1]["y_own"]], 0) for b in range(B)])
    y_sample = np.concatenate([r[c]["y_smp"] for c in range(8)], 0)[:, None, :]
    outs = [y_prompt, y_sample]
    for g in range(3):
        for nm in ("kc", "vc"):
            if g < 2:
                a = np.stack([r[2 * b + 1]["%s%d" % (nm, g)] for b in range(B)])
            else:
                a = np.stack([np.concatenate([r[2 * b]["%s%d" % (nm, g)], r[2 * b + 1]["%s%d" % (nm, g)]], 0) for b in range(B)])
            outs.append(a.reshape(1, B, -1, 8, 128))
    outs.append(np.stack([r[2 * b + 1]["pool_p"] for b in range(B)])[None])
    for g in range(3):
        for nm in ("ks", "vs"):
            a = np.concatenate([r[c]["%s%d" % (nm, g)] for c in range(8)], 0)
            outs.append(a.reshape(1, 32, WIN[g], 8, 128))
    outs.append(np.concatenate([r[c]["pool_s"] for c in range(8)], 0).reshape(1, 32, 15, 2048))
    return tuple(np.ascontiguousarray(o, dtype=np.float32) for o in outs)
```
